# Optimizing a Trainium2 kernel written in Bass

```python
import math
import jax, jax.numpy as jnp
from jax import lax
import numpy as np

D_MODEL = 1024
BATCH = 16
SEQ = 2048
DEPTH = 2
DEC_BATCH = 8
DEC_SEQ = 64
PAST_LEN = 1024

CHUNK = 64
N_MIXERS = 2
N_CONV = (DEPTH + 1) // 2
N_ATTN = DEPTH // 2
CONV_EXPAND = 2
CONV_WIDTH = CONV_EXPAND * D_MODEL
CONV_W = 3
N_HEADS = 16
HEAD_DIM = 64
ATT_WIDTH = N_HEADS * HEAD_DIM
Q_BLOCK = 128
RMS_EPS = 1e-6
FORGET_BIAS_INIT = 3.0
NEG_INF = -1e30

kernel_name = "hybrid_shortconv_fox_stream_step"


def rms_norm(x, g):
    x32 = x.astype(jnp.float32)
    y = x32 * lax.rsqrt(jnp.mean(x32 * x32, axis=-1, keepdims=True) + RMS_EPS)
    return (y * g.astype(jnp.float32)).astype(x.dtype)


def ada_norm(x, c, g, w_ada, b_ada):
    mod = jax.nn.silu(c) @ w_ada + b_ada
    shift, scale, gate = jnp.split(mod, 3, axis=-1)
    h = rms_norm(x, g) * (1.0 + scale[:, None, :]) + shift[:, None, :]
    return h, gate


def conv_mixer(h, hist, w_in, conv_k, w_out):
    bg, cg, xv, z = jnp.split(h @ w_in, 4, axis=-1)
    u = cg * xv
    T = u.shape[1]
    full = jnp.concatenate([hist, u], axis=1)
    conv = (conv_k[0] * full[:, 0:T] + conv_k[1] * full[:, 1:T + 1]
            + conv_k[2] * full[:, 2:T + 2])
    y = bg * conv * jax.nn.silu(z)
    return y @ w_out, full[:, -(CONV_W - 1):]


def attn_proj(h, w_in, b_f):
    B, T, _ = h.shape
    proj = h @ w_in
    q, k, v, z = [proj[..., i * ATT_WIDTH:(i + 1) * ATT_WIDTH] for i in range(4)]
    f_logit = proj[..., 4 * ATT_WIDTH:] + b_f
    logf = jax.nn.log_sigmoid(f_logit.astype(jnp.float32)).astype(h.dtype)
    shp = (B, T, N_HEADS, HEAD_DIM)
    return q.reshape(shp), k.reshape(shp), v.reshape(shp), z, logf


def fox_block(q, cq, qpos, k, v, ck, kpos):
    s = jnp.einsum('bqhd,bkhd->bhqk', q.astype(jnp.float32), k.astype(jnp.float32)) / math.sqrt(HEAD_DIM)
    s = s + jnp.transpose(cq, (0, 2, 1))[..., :, None] - jnp.transpose(ck, (0, 2, 1))[..., None, :]
    mask = kpos[None, :] <= qpos[:, None]
    s = jnp.where(mask[None, None], s, jnp.float32(NEG_INF))
    p = jax.nn.softmax(s, axis=-1)
    o = jnp.einsum('bhqk,bkhd->bqhd', p, v.astype(jnp.float32))
    return o.astype(v.dtype)


def fox_prompt(q, k, v, logf):
    B, S = q.shape[0], q.shape[1]
    nb = S // Q_BLOCK
    cum = jnp.cumsum(logf.astype(jnp.float32), axis=1)
    pos = jnp.arange(S)
    qs = q.reshape(B, nb, Q_BLOCK, N_HEADS, HEAD_DIM).swapaxes(0, 1)
    cqs = cum.reshape(B, nb, Q_BLOCK, N_HEADS).swapaxes(0, 1)
    ps = pos.reshape(nb, Q_BLOCK)

    def body(args):
        qb, cqb, pb = args
        return fox_block(qb, cqb, pb, k, v, cum, pos)

    o = lax.map(body, (qs, cqs, ps))
    return o.swapaxes(0, 1).reshape(B, S, N_HEADS, HEAD_DIM)


def fox_sample(q, k_new, v_new, logf_new, cache_k, cache_v, cache_logf):
    P = cache_k.shape[1]
    T = q.shape[1]
    k_all = jnp.concatenate([cache_k, k_new], axis=1)
    v_all = jnp.concatenate([cache_v, v_new], axis=1)
    cum = jnp.cumsum(jnp.concatenate([cache_logf, logf_new], axis=1).astype(jnp.float32), axis=1)
    kpos = jnp.arange(P + T)
    qpos = P + jnp.arange(T)
    return fox_block(q, cum[:, P:], qpos, k_all, v_all, cum, kpos)


def setup_inputs(seed: int = 0) -> dict:
    key = jax.random.key(seed)
    ks = jax.random.split(key, 20)
    f32 = jnp.float32
    nrm = lambda k, shp, s=1.0: (jax.random.normal(k, shp, f32) * s).astype(f32)
    D, E, A, H = D_MODEL, CONV_WIDTH, ATT_WIDTH, N_HEADS
    return {
        "x_prompt": nrm(ks[0], (BATCH, SEQ, D)),
        "x_sample": nrm(ks[1], (DEC_BATCH, DEC_SEQ, D)),
        "c_prompt": nrm(ks[2], (BATCH, D)),
        "c_sample": nrm(ks[3], (DEC_BATCH, D)),
        "state_conv": nrm(ks[4], (N_CONV, DEC_BATCH, CONV_W - 1, E)),
        "cache_k": nrm(ks[5], (N_ATTN, DEC_BATCH, PAST_LEN, H, HEAD_DIM)),
        "cache_v": nrm(ks[6], (N_ATTN, DEC_BATCH, PAST_LEN, H, HEAD_DIM)),
        "cache_logf": jax.nn.log_sigmoid(FORGET_BIAS_INIT + nrm(ks[7], (N_ATTN, DEC_BATCH, PAST_LEN, H))),
        "norm_g": 1.0 + nrm(ks[8], (DEPTH, D), 0.02),
        "ada_w": nrm(ks[9], (DEPTH, D, 3 * D), D ** -0.5),
        "ada_b": nrm(ks[10], (DEPTH, 3 * D), 0.01),
        "conv_w_in": nrm(ks[11], (N_CONV, D, 4 * E), D ** -0.5),
        "conv_k": nrm(ks[12], (N_CONV, CONV_W, E), CONV_W ** -0.5),
        "conv_w_out": nrm(ks[13], (N_CONV, E, D), E ** -0.5),
        "attn_w_in": nrm(ks[14], (N_ATTN, D, 4 * A + H), D ** -0.5),
        "attn_b_f": FORGET_BIAS_INIT + nrm(ks[15], (N_ATTN, H), 0.1),
        "attn_w_out": nrm(ks[16], (N_ATTN, A, D), A ** -0.5),
        "final_g": 1.0 + nrm(ks[17], (D,), 0.02),
    }


def reference(x_prompt, x_sample, c_prompt, c_sample, state_conv, cache_k, cache_v, cache_logf,
              norm_g, ada_w, ada_b, conv_w_in, conv_k, conv_w_out, attn_w_in, attn_b_f, attn_w_out, final_g):
    xp, xs = x_prompt, x_sample
    Bp, Bs = xp.shape[0], xs.shape[0]
    conv_p, conv_s = [], []
    kp_l, vp_l, fp_l, ks_l, vs_l, fs_l = [], [], [], [], [], []
    for i in range(DEPTH):
        j = i // N_MIXERS
        hp, gp = ada_norm(xp, c_prompt, norm_g[i], ada_w[i], ada_b[i])
        hs, gs = ada_norm(xs, c_sample, norm_g[i], ada_w[i], ada_b[i])
        if i % N_MIXERS == 0:
            zero_hist = jnp.zeros((Bp, CONV_W - 1, CONV_WIDTH), hp.dtype)
            op, hist_p = conv_mixer(hp, zero_hist, conv_w_in[j], conv_k[j], conv_w_out[j])
            os_, hist_s = conv_mixer(hs, state_conv[j].astype(hs.dtype), conv_w_in[j], conv_k[j], conv_w_out[j])
            conv_p.append(hist_p)
            conv_s.append(hist_s)
        else:
            qp, kp, vp, zp, lfp = attn_proj(hp, attn_w_in[j], attn_b_f[j])
            ap = fox_prompt(qp, kp, vp, lfp).reshape(Bp, -1, ATT_WIDTH)
            op = (ap * jax.nn.silu(zp)) @ attn_w_out[j]
            qs, ks_, vs_, zs, lfs = attn_proj(hs, attn_w_in[j], attn_b_f[j])
            as_ = fox_sample(qs, ks_, vs_, lfs, cache_k[j], cache_v[j], cache_logf[j]).reshape(Bs, -1, ATT_WIDTH)
            os_ = (as_ * jax.nn.silu(zs)) @ attn_w_out[j]
            kp_l.append(kp); vp_l.append(vp); fp_l.append(lfp)
            ks_l.append(ks_); vs_l.append(vs_); fs_l.append(lfs)
        xp = xp + gp[:, None, :] * op
        xs = xs + gs[:, None, :] * os_
    y_prompt = rms_norm(xp, final_g)
    y_sample = rms_norm(xs, final_g)
    new_conv_prompt = jnp.stack(conv_p)
    new_k_prompt = jnp.stack(kp_l)
    new_v_prompt = jnp.stack(vp_l)
    new_logf_prompt = jnp.stack(fp_l)
    new_conv_sample = jnp.stack(conv_s)
    new_k_sample = jnp.stack(ks_l)
    new_v_sample = jnp.stack(vs_l)
    new_logf_sample = jnp.stack(fs_l)
    return (y_prompt, y_sample, new_conv_prompt, new_k_prompt, new_v_prompt, new_logf_prompt,
            new_conv_sample, new_k_sample, new_v_sample, new_logf_sample)
```

```python
import numpy as np
import ml_dtypes
import concourse.bass as bass
import concourse.mybir as mybir
from concourse.bass_utils import run_bass_kernel_spmd
from contextlib import ExitStack

F32 = mybir.dt.float32
BF16 = mybir.dt.bfloat16
AF = mybir.ActivationFunctionType
ALU = mybir.AluOpType

N_DMA_SEMS = 32
D = 1024
E = 2048
NTP = 4096
NTS = 64
NTOK = NTP + NTS
PAST = 1024
H = 16
DH = 64
EPS = 1e-6
NEG = -30000.0


class Prog:
    def __init__(self, nc):
        self.nc = nc
        self.ops = []
        self.pending_st = []

    def op(self, eng, fn, reads=(), writes=(), dma=False):
        self.ops.append(dict(eng=eng, fn=fn, reads=tuple(reads), writes=tuple(writes), dma=dma,
                             deps=set(), signal=False, barrier=False))

    def pe(self, fn, reads=(), writes=()):
        self.op('pe', fn, reads, writes)

    def act(self, fn, reads=(), writes=()):
        self.op('act', fn, reads, writes)

    def dve(self, fn, reads=(), writes=()):
        self.op('dve', fn, reads, writes)

    def pool(self, fn, reads=(), writes=()):
        self.op('pool', fn, reads, writes)

    def dma(self, eng, out, in_, reads=(), writes=(), carry=False, **kw):
        self.op(eng, lambda e: e.dma_start(out=out, in_=in_, **kw), reads, writes, dma=True)
        self.ops[-1]['carry'] = carry

    def store(self, out, in_, reads=(), writes=(), **kw):
        self.pending_st.append((out, in_, tuple(reads), tuple(writes), kw))

    def flush(self):
        for out, in_, reads, writes, kw in self.pending_st:
            self.dma('sp', out, in_, reads=reads, writes=writes, **kw)
        self.pending_st = []

    def barrier(self):
        self.flush()
        for e in ('pe', 'act', 'dve', 'pool', 'sp'):
            self.ops.append(dict(eng=e, fn=None, reads=(), writes=(), dma=False, deps=set(),
                                 signal=False, barrier=True))

    def analyze(self):
        ops = self.ops
        last_writer = {}
        readers = {}
        last_on_eng = {}
        dmas_since = []
        i = 0
        n = len(ops)
        while i < n:
            op = ops[i]
            if op['barrier']:
                carried = set(d for d in dmas_since if ops[d].get('carry'))
                deps = set(last_on_eng.values()) | (set(dmas_since) - carried)
                j = i
                while j < n and ops[j]['barrier']:
                    ops[j]['deps'] = set(deps)
                    j += 1
                last_writer = {k: v for k, v in last_writer.items() if v in carried}
                readers = {}
                dmas_since = sorted(carried)
                for d in carried:
                    ops[d]['carry'] = False
                i = j
                continue
            deps = set()
            raw = set()
            for k in op['reads']:
                if k in last_writer:
                    deps.add(last_writer[k])
                    raw.add(last_writer[k])
            for k in op['writes']:
                if k in last_writer:
                    deps.add(last_writer[k])
                deps.update(readers.get(k, {}).values())
            deps.discard(i)
            op['deps'] = deps
            op['raw'] = raw
            for k in op['reads']:
                rk = ('dma', i) if op['dma'] else op['eng']
                readers.setdefault(k, {})[rk] = i
            for k in op['writes']:
                last_writer[k] = i
                readers[k] = {}
            if op['dma']:
                dmas_since.append(i)
            else:
                last_on_eng[op['eng']] = i
            i += 1
        dma_count = [0] * N_DMA_SEMS
        dma_last = [None] * N_DMA_SEMS
        jq = {'sp': 0, 'pool': 0}
        half_n = N_DMA_SEMS // 2
        for i, op in enumerate(ops):
            if op['dma']:
                qn = op['eng']
                s = (jq[qn] % half_n) + (0 if qn == 'sp' else half_n)
                jq[qn] += 1
                if dma_last[s] is not None:
                    op['deps'].add(dma_last[s])
                dma_count[s] += 16
                op['dsem'] = s
                op['dval'] = dma_count[s]
                dma_last[s] = i
        self.dma_final = dma_count
        for i, op in enumerate(ops):
            need = []
            for d in op['deps']:
                od = ops[d]
                if od['dma']:
                    need.append(d)
                    continue
                if od['eng'] == op['eng']:
                    if op['eng'] == 'pe' and not op['barrier']:
                        continue
                    if op['dma']:
                        continue
                    if not op['barrier'] and d not in op.get('raw', ()):
                        continue
                need.append(d)
                od['signal'] = True
            op['need'] = need
        cnt = {}
        for op in ops:
            if op['signal'] and not op['dma']:
                cnt[op['eng']] = cnt.get(op['eng'], 0) + 1
                op['sigval'] = cnt[op['eng']]

    def emit(self):
        nc = self.nc
        self.analyze()
        ops = self.ops
        engs = ['pe', 'act', 'dve', 'pool', 'sp']
        with ExitStack() as es:
            esem = {e: es.enter_context(nc.semaphore('prog_' + e)) for e in engs}
            dsem = [es.enter_context(nc.semaphore('dma%d' % i)) for i in range(N_DMA_SEMS)]
            block = es.enter_context(nc.Block())

            def run_engine(ename, eng):
                known = {}
                for op in ops:
                    if op['eng'] != ename:
                        continue
                    waits = {}
                    for d in op['need']:
                        od = ops[d]
                        if od['dma']:
                            key = ('d', od['dsem'])
                            val = od['dval']
                        else:
                            key = ('e', od['eng'])
                            val = od['sigval']
                        if known.get(key, 0) >= val:
                            continue
                        waits[key] = max(waits.get(key, 0), val)
                    for key, val in waits.items():
                        sem = dsem[key[1]] if key[0] == 'd' else esem[key[1]]
                        eng.wait_ge(sem, val)
                        known[key] = val
                    if op['fn'] is None:
                        continue
                    ins = op['fn'](eng)
                    if op['dma']:
                        ins.then_inc(dsem[op['dsem']], 16)
                    elif op['signal']:
                        ins.then_inc(esem[ename], 1)
                if ename == 'sp':
                    for s in range(N_DMA_SEMS):
                        if self.dma_final[s] > 0:
                            eng.wait_ge(dsem[s], self.dma_final[s])

            @block.tensor
            def _(e):
                run_engine('pe', e)

            @block.scalar
            def _(e):
                run_engine('act', e)

            @block.vector
            def _(e):
                run_engine('dve', e)

            @block.gpsimd
            def _(e):
                run_engine('pool', e)

            @block.sync
            def _(e):
                run_engine('sp', e)


def build_nc(debug=False, phases=3, dbg=None):
    dbg = dbg or {}
    nc = bass.Bass("TRN2", target_bir_lowering=False)

    def din(name, shape, dt=F32):
        return nc.dram_tensor(name, list(shape), dt, kind="ExternalInput").ap()

    def dout(name, shape, dt=F32):
        return nc.dram_tensor(name, list(shape), dt, kind="ExternalOutput").ap()

    def dscr(name, shape, dt):
        return nc.dram_tensor(name, list(shape), dt, kind=("ExternalOutput" if debug else "Internal")).ap()

    xp = din("xp", [NTP, D]); xs = din("xs", [NTS, D]); cc = din("cc", [3, D])
    sconv = din("sconv", [2, E]); ck = din("ck", [PAST, D]); cv = din("cv", [PAST, D])
    clf = din("clf", [PAST, H]); norm_g = din("norm_g", [2, D]); ada_w = din("ada_w", [2, D, 3 * D])
    ada_b = din("ada_b", [2, 3 * D]); w_in0 = din("w_in0", [D, 4 * E]); convk = din("convk", [3, E])
    w_out0 = din("w_out0", [E, D]); w_in1 = din("w_in1", [D, 4 * D + H]); b_f = din("b_f", [1, H])
    w_out1 = din("w_out1", [D, D]); final_g = din("final_g", [1, D])
    c_ident = din("c_ident", [128, 128]); c_tri = din("c_tri", [128, 128])
    c_mask = din("c_mask", [128, 128]); c_sel = din("c_sel", [3, 3 * 128]); c_mask2 = din("c_mask2", [128, 128])
    c_selq = din("c_selq", [96, H * 128])

    y_p = dout("y_p", [NTP, D]); y_s = dout("y_s", [NTS, D]); conv_p = dout("conv_p", [2, 2, E])
    k_p = dout("k_p", [NTP, D]); v_p = dout("v_p", [NTP, D]); lf_p = dout("lf_p", [NTP, H])
    conv_s = dout("conv_s", [2, E]); k_s = dout("k_s", [NTS, D]); v_s = dout("v_s", [NTS, D])
    lf_s = dout("lf_s", [NTS, H])

    x1a_d = dscr("x1a_d", [NTOK, D], F32)
    x1_d = dscr("x1_d", [NTOK, D], F32)
    q_d = dscr("q_d", [NTOK, D], BF16)
    sz_d = dscr("sz_d", [NTOK, D], BF16)

    def tokrows(ap_p, ap_s, r0, rows):
        if r0 >= NTP:
            return ap_s[r0 - NTP:r0 - NTP + rows, :]
        return ap_p[r0:r0 + rows, :]

    P = Prog(nc)
    with ExitStack() as es0:
        _uid = [0]

        def sb(name, shape, dt, es=es0):
            _uid[0] += 1
            return es.enter_context(nc.sbuf_tensor("%s_%d" % (name, _uid[0]), list(shape), dt))

        def ps(name, shape, dt, es=es0):
            return es.enter_context(nc.psum_tensor(name, list(shape), dt))

        pbT = ps("pbT", [128, 8, 128], BF16)
        pbk = [ps("pb%d" % i, [128, 512], F32) for i in range(7)]
        KT = 'psT'
        KB = ['ps%d' % i for i in range(7)]

        identf = sb("identf", [128, 128], F32)
        identb = sb("identb", [128, 128], BF16)
        trif = sb("trif", [128, 128], F32)
        onesf = sb("onesf", [128, 128], F32)
        maskb = sb("maskb", [128, 128], BF16)
        maskb2 = sb("maskb2", [128, 128], BF16)
        self3 = sb("sel3", [3, 3 * 128], F32)
        epsT = sb("epsT", [128, 1], F32)
        nhalf = sb("nhalf", [128, 1], F32)
        sc1T = sb("sc1T", [128, 2, 8, 3], F32)
        shT = sb("shT", [128, 2, 8, 3], F32)
        gate_bc = sb("gate_bc", [128, 2, 3, D], F32)
        NKT = 33 + 8
        negcum = sb("negcum", [128, NKT, H], F32)
        cbc = sb("cbc", [128, NKT + 1, H], F32)
        cmid = sb("cmid", [128, NKT, H], F32)
        augk = sb("augk", [96, NKT * 128], BF16)
        selq = sb("selq", [96, H, 128], BF16)

        P.dma('sp', identf[:], c_ident[:, :], writes=['identf'])
        P.dma('sp', trif[:], c_tri[:, :], writes=['trif'])
        P.dma('sp', self3[:], c_sel[:, :], writes=['sel3'])
        P.dma('pool', maskb[:], c_mask[:, :], writes=['maskb'])
        P.dma('pool', maskb2[:], c_mask2[:, :], writes=['maskb2'])
        P.dma('pool', selq[:].rearrange("p h q -> p (h q)"), c_selq[:, :], writes=['selq'])
        P.pool(lambda e: e.memset(augk[:], 0.0), writes=['augk'])
        P.dve(lambda e: e.tensor_copy(out=identb[:], in_=identf[:]), reads=['identf'], writes=['identb'])
        P.dve(lambda e: e.memset(onesf[:], 1.0), writes=['onesf'])
        P.dve(lambda e: e.memset(epsT[:], EPS), writes=['epsT'])
        P.pool(lambda e: e.memset(nhalf[:], -0.5), writes=['nhalf'])

        def load_w0_chunk(w0, half, c2, carry):
            for j in (2, 3, 1, 0):
                c0 = j * E + half * 1024 + c2 * 512
                P.dma('pool', w0[:, :, j, c2 * 512:(c2 + 1) * 512],
                      w_in0[:, c0:c0 + 512].rearrange("(k p) n -> p k n", p=128), writes=[('w0', c2)], carry=carry)

        def load_wo0(wo0, half, carry):
            P.dma('pool', wo0[:], w_out0[half * 1024:(half + 1) * 1024, :].rearrange("(e p) n -> p e n", p=128),
                  writes=['wo0'], carry=carry)

        def load_w0(w0, wo0, half, carry):
            for c2 in range(2):
                load_w0_chunk(w0, half, c2, carry)
            load_wo0(wo0, half, carry)

        es_w = ExitStack()
        w0_h0 = sb("w0", [128, 8, 4, 1024], BF16, es_w)
        wo0_h0 = sb("wo0", [128, 8, 1024], BF16, es_w)
        load_w0(w0_h0, wo0_h0, 0, True)

        with ExitStack() as es:
            cT = sb("cT", [128, 8, 3], F32, es)
            sT = sb("sT", [128, 8, 3], F32, es)
            gT = sb("gT", [128, 2, 8], F32, es)
            mod = sb("mod", [3, 2, 3 * D], F32, es)
            adab_b = [sb("adab%d" % i, [3, 512], F32, es) for i in range(2)]
            wblk = [sb("wblk%d" % i, [128, 8, 512], F32, es) for i in range(2)]
            for b3 in range(3):
                P.dma('sp', cT[:, :, b3], cc[b3, :].rearrange("(k p) -> p k", p=128), writes=['cT'],
                      allow_slow_non_contiguous=True)
            for l2 in range(2):
                P.dma('sp', gT[:, l2, :], norm_g[l2, :].rearrange("(k p) -> p k", p=128), writes=['gT'],
                      allow_slow_non_contiguous=True)
            P.act(lambda e: e.activation(out=sT[:], in_=cT[:], func=AF.Silu), reads=['cT'], writes=['sT'])
            bi = 0
            for l in range(2):
                for nb in range(6):
                    wb = wblk[bi % 2]
                    wk = 'wblk%d' % (bi % 2)
                    bi += 1
                    P.dma('sp', wb[:], ada_w[l, :, nb * 512:(nb + 1) * 512].rearrange("(k p) n -> p k n", p=128),
                          writes=[wk])
                    ab_ = adab_b[(bi - 1) % 2]; abk = 'adab%d' % ((bi - 1) % 2)
                    P.dma('sp', ab_[:], ada_b[l:l + 1, nb * 512:(nb + 1) * 512].to_broadcast([3, 512]), writes=[abk])
                    for k in range(8):
                        P.pe(lambda e, k=k, wb=wb: e.matmul(pbk[0][0:3, :], lhsT=sT[:, k, :], rhs=wb[:, k, :],
                                                          start=(k == 0), stop=(k == 7)),
                             reads=['sT', wk], writes=[KB[0]])
                    P.dve(lambda e, l=l, nb=nb, ab_=ab_: e.tensor_tensor(out=mod[:, l, nb * 512:(nb + 1) * 512],
                                                                         in0=pbk[0][0:3, :], in1=ab_[:, :], op=ALU.add),
                          reads=[KB[0], abk], writes=['mod'])
            pmod = pbk[1][:, 0:96].rearrange("p (l j b) -> p l j b", l=2, j=16, b=3)
            for l in range(2):
                for j in range(16):
                    P.pe(lambda e, l=l, j=j: e.transpose(out=pmod[:, l, j, :], in_=mod[0:3, l, j * 128:(j + 1) * 128],
                                                         identity=identf[0:3, 0:3]),
                         reads=['mod', 'identf'], writes=[KB[1]])
            for l in range(2):
                P.dve(lambda e, l=l: e.tensor_copy(out=shT[:, l, :, :], in_=pmod[:, l, 0:8, :]),
                      reads=[KB[1]], writes=['shT'])
                P.dve(lambda e, l=l: e.scalar_tensor_tensor(out=sc1T[:, l, :, :], in0=pmod[:, l, 8:16, :], scalar=1.0,
                                                            in1=gT[:, l, :].unsqueeze(2).to_broadcast([128, 8, 3]),
                                                            op0=ALU.add, op1=ALU.mult),
                      reads=[KB[1], 'gT'], writes=['sc1T'])
            for l in range(2):
                for b in range(3):
                    for hf in range(2):
                        P.pe(lambda e, l=l, b=b, hf=hf: e.matmul(pbk[2][:, :], lhsT=self3[0:3, b * 128:(b + 1) * 128],
                                                                 rhs=mod[0:3, l, 2 * D + hf * 512:2 * D + (hf + 1) * 512],
                                                                 start=True, stop=True),
                             reads=['sel3', 'mod'], writes=[KB[2]])
                        P.act(lambda e, l=l, b=b, hf=hf: e.activation(out=gate_bc[:, l, b, hf * 512:(hf + 1) * 512],
                                                                      in_=pbk[2][:, :], func=AF.Copy),
                              reads=[KB[2]], writes=['gate_bc'])
            P.barrier()

        def norm_stats(xt, rows, xn, ssq, xkey, tagn):
            P.act(lambda e: e.activation(out=xn[0:rows, :], in_=xt[0:rows, :], func=AF.Square,
                                         accum_out=ssq[0:rows, 0:1]),
                 reads=[xkey], writes=[tagn + 'xn', tagn + 'ss'])
            P.pool(lambda e: e.tensor_scalar(out=ssq[0:rows, 1:2], in0=ssq[0:rows, 0:1], scalar1=1.0 / D, scalar2=EPS,
                                             op0=ALU.mult, op1=ALU.add),
                   reads=[tagn + 'ss'], writes=[tagn + 'ss'])
            P.pool(lambda e: e.tensor_tensor(out=ssq[0:rows, 2:3], in0=ssq[0:rows, 1:2], in1=nhalf[0:rows, 0:1], op=ALU.pow),
                   reads=[tagn + 'ss', 'nhalf'], writes=[tagn + 'ss'])
            P.dve(lambda e: e.tensor_scalar(out=xn[0:rows, :], in0=xt[0:rows, :], scalar1=ssq[0:rows, 2:3],
                                            scalar2=None, op0=ALU.mult),
                  reads=[xkey, tagn + 'ss', tagn + 'xn'], writes=[tagn + 'xn'])

        def norm_tr(rows, l, b, xn, hT_dst, hkeys_w, tagn):
            for k in range(8):
                P.pe(lambda e, k=k: e.transpose(out=pbT[:, k, 0:rows], in_=xn[0:rows, k * 128:(k + 1) * 128],
                                                identity=identb[0:rows, 0:rows]),
                     reads=[tagn + 'xn', 'identb'], writes=[KT])
            for k in range(8):
                if k % 2 == 0:
                    P.act(lambda e, k=k: e.activation(out=hT_dst[:, k, 0:rows], in_=pbT[:, k, 0:rows], func=AF.Identity,
                                                      scale=sc1T[:, l, k, b:b + 1], bias=shT[:, l, k, b:b + 1]),
                          reads=[KT, 'sc1T', 'shT'], writes=hkeys_w)
                else:
                    P.dve(lambda e, k=k: e.tensor_scalar(out=hT_dst[:, k, 0:rows], in0=pbT[:, k, 0:rows],
                                                         scalar1=sc1T[:, l, k, b:b + 1], scalar2=shT[:, l, k, b:b + 1],
                                                         op0=ALU.mult, op1=ALU.add),
                          reads=[KT, 'sc1T', 'shT'], writes=hkeys_w)

        if phases == 0:
            P.emit()
            return nc
        groups = ([(g * 256, 256, 0) for g in range(8)] + [(NTP, NTS, 2)] +
                  [(g * 256, 256, 1) for g in range(8, 16)])
        es_p1 = ExitStack()
        p1 = {}
        for half in range(2):
            with ExitStack() as es:
                w0, wo0 = w0_h0, wo0_h0
                if half == 0:
                    p1['ckT'] = [sb("ckT%d" % i, [128, 3, 8], F32, es_p1) for i in range(2)]
                    p1['u_all'] = sb("u_all", [128, 8, 258], F32, es_p1)
                    p1['xt_b'] = [sb("xt%d" % i, [128, D], F32, es_p1) for i in range(2)]
                    p1['xn_b'] = [sb("xn%d" % i, [128, D], BF16, es_p1) for i in range(2)]
                    p1['ss_b'] = [sb("ss%d" % i, [128, 4], F32, es_p1) for i in range(2)]
                    p1['hT_b'] = [sb("hT%d" % i, [128, 8, 256], BF16, es_p1) for i in range(2)]
                    p1['yT_b'] = [sb("yT%d" % i, [128, 8, 256], BF16, es_p1) for i in range(2)]
                    p1['xv_b'] = [sb("xv%d" % i, [128, 256], F32, es_p1) for i in range(3)]
                    p1['sz_b'] = [sb("szz%d" % i, [128, 256], F32, es_p1) for i in range(3)]
                    p1['c1_b'] = [sb("c1%d" % i, [128, 256], F32, es_p1) for i in range(3)]
                    p1['t_b'] = [sb("tt%d" % i, [128, 256], F32, es_p1) for i in range(3)]
                    p1['xr_b'] = [sb("xr%d" % i, [128, D], F32, es_p1) for i in range(2)]
                    p1['xo_b'] = [sb("xo%d" % i, [128, D], F32, es_p1) for i in range(2)]
                ckT = p1['ckT'][half]; u_all = p1['u_all']; xt_b = p1['xt_b']; xn_b = p1['xn_b']; ss_b = p1['ss_b']
                hT_b = p1['hT_b']; yT_b = p1['yT_b']; xv_b = p1['xv_b']; sz_b = p1['sz_b']; c1_b = p1['c1_b']
                t_b = p1['t_b']; zc_b = p1['t_b']; xr_b = p1['xr_b']; xo_b = p1['xo_b']
                for j3 in range(3):
                    P.dma('sp', ckT[:, j3, :], convk[j3, half * 1024:(half + 1) * 1024].rearrange("(e p) -> p e", p=128),
                          writes=['ckT%d' % half], allow_slow_non_contiguous=True)
                src_p, src_s = (xp, xs) if half == 0 else (x1a_d[0:NTP, :], x1a_d[NTP:NTOK, :])
                dst = x1a_d if half == 0 else x1_d
                UK = [('u', e8) for e8 in range(8)]
                cnt = {'u': 0, 't': 0}

                def do_stats(gi):
                    r0, N, b = groups[gi]
                    for ti in range((N + 127) // 128):
                        rows = min(128, N - ti * 128)
                        bi = ti % 2
                        xt = xt_b[bi]; xk = 'xt%d' % bi
                        P.dma('sp', xt[0:rows, :], tokrows(xp, xs, r0 + ti * 128, rows), writes=[xk])
                        norm_stats(xt, rows, xn_b[bi], ss_b[bi], xk, 'n%d' % bi)

                def do_tr(gi, tiles=None):
                    r0, N, b = groups[gi]
                    hT = hT_b[gi % 2]; hk = 'hT%d' % (gi % 2)
                    for ti in range((N + 127) // 128):
                        if tiles is not None and ti not in tiles:
                            continue
                        rows = min(128, N - ti * 128)
                        bi = ti % 2
                        norm_tr(rows, 0, b, xn_b[bi], hT[:, :, ti * 128: ti * 128 + 128], [hk], 'n%d' % bi)

                def do_units(gi, mid_hook=None):
                    r0, N, b = groups[gi]
                    hT = hT_b[gi % 2]; hk = 'hT%d' % (gi % 2)
                    yT = yT_b[gi % 2]; yk = 'yT%d' % (gi % 2)
                    if r0 % 2048 == 0:
                        if b < 2:
                            P.dve(lambda e: e.memset(u_all[:, :, 0:2], 0.0), writes=UK)
                        else:
                            for j2 in range(2):
                                P.dma('sp', u_all[:, :, j2], sconv[j2, half * 1024:(half + 1) * 1024].rearrange("(e p) -> p e", p=128),
                                      writes=UK, allow_slow_non_contiguous=True)
                    for e8 in range(8):
                        if mid_hook is not None:
                            mid_hook(e8)
                        uc = cnt['u']; cnt['u'] += 1
                        s3 = uc % 3
                        bX = pbk[s3 * 2]; kX = KB[s3 * 2]
                        bY = pbk[s3 * 2 + 1]; kY = KB[s3 * 2 + 1]
                        uk = ('u', e8)
                        xv = xv_b[s3]; xvk = 'xv%d' % s3
                        sz = sz_b[s3]; szk = 'sz%d' % s3
                        c1 = c1_b[s3]; c1k = 'c1%d' % s3
                        tt = t_b[s3]; ttk = 'tt%d' % s3
                        zc = zc_b[s3]; zck = 'zc%d' % s3
                        for j, (bank, kb, off) in enumerate([(bY, kY, 0), (bY, kY, 256), (bX, kX, 256), (bX, kX, 0)]):
                            jj = [2, 3, 1, 0][j]
                            for k in range(8):
                                P.pe(lambda e, jj=jj, k=k, bank=bank, off=off, e8=e8, hT=hT, N=N:
                                     e.matmul(bank[:, off:off + N], lhsT=w0[:, k, jj, e8 * 128:(e8 + 1) * 128],
                                              rhs=hT[:, k, 0:N], start=(k == 0), stop=(k == 7)),
                                     reads=[('w0', e8 // 4), hk], writes=[kb])
                        P.act(lambda e, bY=bY, xv=xv, N=N: e.activation(out=xv[:, 0:N], in_=bY[:, 0:N], func=AF.Copy),
                              reads=[kY], writes=[xvk])
                        P.act(lambda e, bY=bY, sz=sz, N=N: e.activation(out=sz[:, 0:N], in_=bY[:, 256:256 + N], func=AF.Silu),
                              reads=[kY], writes=[szk])
                        P.dve(lambda e, bX=bX, xv=xv, e8=e8, N=N: e.tensor_tensor(out=u_all[:, e8, 2:2 + N], in0=bX[:, 256:256 + N],
                                                                                  in1=xv[:, 0:N], op=ALU.mult),
                              reads=[kX, xvk, uk], writes=[uk])
                        P.act(lambda e, c1=c1, e8=e8, N=N, ckT=ckT: e.activation(out=c1[:, 0:N], in_=u_all[:, e8, 2:2 + N], func=AF.Copy,
                                                                        scale=ckT[:, 2, e8:e8 + 1]),
                              reads=[uk, 'ckT%d' % half], writes=[c1k])
                        P.dve(lambda e, bX=bX, sz=sz, tt=tt, N=N: e.tensor_tensor(out=tt[:, 0:N], in0=bX[:, 0:N],
                                                                                 in1=sz[:, 0:N], op=ALU.mult),
                              reads=[kX, szk], writes=[ttk])
                        P.dve(lambda e, c1=c1, e8=e8, N=N, ckT=ckT: e.scalar_tensor_tensor(
                            out=c1[:, 0:N], in0=u_all[:, e8, 1:1 + N], scalar=ckT[:, 1, e8:e8 + 1], in1=c1[:, 0:N],
                            op0=ALU.mult, op1=ALU.add), reads=[uk, c1k, 'ckT%d' % half], writes=[c1k])
                        P.dve(lambda e, c1=c1, e8=e8, N=N, ckT=ckT: e.scalar_tensor_tensor(
                            out=c1[:, 0:N], in0=u_all[:, e8, 0:N], scalar=ckT[:, 0, e8:e8 + 1], in1=c1[:, 0:N],
                            op0=ALU.mult, op1=ALU.add), reads=[uk, c1k, 'ckT%d' % half], writes=[c1k])
                        P.dve(lambda e, tt=tt, c1=c1, yT=yT, e8=e8, N=N: e.tensor_tensor(out=yT[:, e8, 0:N], in0=tt[:, 0:N],
                                                                                        in1=c1[:, 0:N], op=ALU.mult),
                              reads=[ttk, c1k], writes=[yk])
                        P.act(lambda e, e8=e8, N=N: e.activation(out=u_all[:, e8, 0:2], in_=u_all[:, e8, N:N + 2], func=AF.Copy),
                              reads=[uk], writes=[uk])
                    if ((r0 + N) % 2048 == 0) or b == 2:
                        cdst = conv_p[b, :, half * 1024:(half + 1) * 1024] if b < 2 else conv_s[:, half * 1024:(half + 1) * 1024]
                        for j2 in range(2):
                            P.store(cdst[j2, :].rearrange("(e p) -> p e", p=128), u_all[:, :, j2], reads=UK,
                                    writes=['convout'], allow_slow_non_contiguous=True)

                def outproj_steps(gi):
                    r0, N, b = groups[gi]
                    yT = yT_b[gi % 2]; yk = 'yT%d' % (gi % 2)
                    steps = []
                    for ti in range((N + 127) // 128):
                        rows = min(128, N - ti * 128)
                        for hf in range(2):
                            def step(ti=ti, hf=hf, rows=rows):
                                if hf == 0:
                                    st['tc'] = cnt['t']; cnt['t'] += 1
                                tc = st['tc']
                                xr = xr_b[tc % 2]; xrk = 'xr%d' % (tc % 2)
                                xo = xo_b[tc % 2]; xok = 'xo%d' % (tc % 2)
                                if hf == 0:
                                    P.dma('sp', xr[0:rows, :], tokrows(src_p, src_s, r0 + ti * 128, rows), writes=[xrk],
                                          reads=([('x1a', r0 + ti * 128)] if half == 1 else []))
                                bank = pbk[6]; kb = KB[6]
                                for e8 in range(8):
                                    P.pe(lambda e, e8=e8: e.matmul(bank[0:rows, :], lhsT=yT[:, e8, ti * 128: ti * 128 + rows],
                                                                   rhs=wo0[:, e8, hf * 512:(hf + 1) * 512],
                                                                   start=(e8 == 0), stop=(e8 == 7)),
                                         reads=[yk, 'wo0'], writes=[kb])
                                P.dve(lambda e: e.tensor_tensor(out=xo[0:rows, hf * 512:(hf + 1) * 512], in0=bank[0:rows, :],
                                                                in1=gate_bc[0:rows, 0, b, hf * 512:(hf + 1) * 512], op=ALU.mult),
                                      reads=[kb, 'gate_bc'], writes=[xok])
                                if hf == 1:
                                    P.dve(lambda e: e.tensor_tensor(out=xo[0:rows, :], in0=xo[0:rows, :], in1=xr[0:rows, :],
                                                                    op=ALU.add), reads=[xok, xrk], writes=[xok])
                                    P.store(dst[r0 + ti * 128: r0 + ti * 128 + rows, :], xo[0:rows, :], reads=[xok],
                                            writes=[('x1a' if half == 0 else 'x1', r0 + ti * 128)])
                            steps.append(step)
                    return steps

                st = {'tc': 0}
                do_stats(0)
                do_tr(0)
                for gi in range(len(groups)):
                    nxt = gi + 1 < len(groups)
                    if nxt:
                        do_stats(gi + 1)
                    P.flush()

                    psteps = outproj_steps(gi - 1) if gi > 0 else []

                    def hook(e8, gi=gi, nxt=nxt, psteps=psteps):
                        slot = {1: 0, 2: 1, 3: 2, 5: 3}.get(e8)
                        if slot is not None and slot < len(psteps):
                            psteps[slot]()
                        if half == 0 and not nxt and e8 == 4:
                            load_w0_chunk(w0, 1, 0, False)
                        if nxt and e8 == 4:
                            do_tr(gi + 1, (0,))
                        if nxt and e8 == 6:
                            do_tr(gi + 1, (1,))
                    do_units(gi, hook)
                if half == 0:
                    load_w0_chunk(w0, 1, 1, False)
                P.flush()
                for step in outproj_steps(len(groups) - 1):
                    step()
                if half == 0:
                    P.flush()
                    load_wo0(wo0, 1, False)
                else:
                    P.barrier()
                    es_p1.close()
                    es_w.close()
        if phases == 1:
            P.emit()
            return nc
        with ExitStack() as es:
            w1 = sb("w1", [128, 8, 4 * D + H], BF16, es)
            bfb = sb("bfb", [128, H], F32, es)
            lsum = sb("lsum", [128, H], F32, es)
            xt_b = [sb("p2xt%d" % i, [128, D], F32, es) for i in range(2)]
            xn_b = [sb("p2xn%d" % i, [128, D], BF16, es) for i in range(2)]
            ss_b = [sb("p2ss%d" % i, [128, 4], F32, es) for i in range(2)]
            h1_b = [sb("p2h%d" % i, [128, 8, 128], BF16, es) for i in range(2)]
            qb_b = [sb("p2q%d" % i, [128, D], BF16, es) for i in range(2)]
            zb_b = [sb("p2z%d" % i, [128, D], BF16, es) for i in range(2)]
            kf_b = [sb("p2k%d" % i, [128, D], F32, es) for i in range(2)]
            vf_b = [sb("p2v%d" % i, [128, D], F32, es) for i in range(2)]
            fl_b = [sb("p2f%d" % i, [128, 4, H], F32, es) for i in range(2)]
            lf3_b = [sb("p2lf3%d" % i, [128, 96], F32, es) for i in range(2)]
            ct_b = [sb("p2ct%d" % i, [96, 128], F32, es) for i in range(2)]
            r1_b = [sb("p2r1%d" % i, [96, 128], F32, es) for i in range(2)]
            hib_b = [sb("p2hib%d" % i, [96, 128], BF16, es) for i in range(2)]
            lob_b = [sb("p2lob%d" % i, [96, 128], BF16, es) for i in range(2)]
            carry96 = sb("p2carry", [96, 1], F32, es)
            for i2 in range(2):
                P.pool(lambda e, i2=i2: e.memset(lf3_b[i2][:], 0.0), writes=['p2lf3%d' % i2])
            for cb in [8] + list(range(8)):
                c0, c1_ = cb * 512, min((cb + 1) * 512, 4 * D + H)
                P.dma('pool', w1[:, :, c0:c1_], w_in1[:, c0:c1_].rearrange("(k p) n -> p k n", p=128), writes=[('w1', cb)])
            P.dma('sp', bfb[:], b_f[0:1, :].to_broadcast([128, H]), writes=['bfb'])
            tl = [('tok', t * 128, 128, t // 16, t) for t in range(32)]
            tl += [('cache', j * 128, 128, 2, 32 + j) for j in range(8)]
            tl += [('tok', NTP, NTS, 2, 40)]
            toks = [i for i, t in enumerate(tl) if t[0] == 'tok']
            tokpos = {i: n for n, i in enumerate(toks)}
            cntb = {'bc': 0}

            def p2_stats(ti):
                kind, r0, rows, b, kt = tl[ti]
                par = tokpos[ti] % 2
                xt = xt_b[par]; xk = 'p2xt%d' % par
                P.dma('sp', xt[0:rows, :], x1_d[r0:r0 + rows, :], writes=[xk])
                norm_stats(xt, rows, xn_b[par], ss_b[par], xk, 'p2n%d' % par)

            def p2_tr(ti):
                kind, r0, rows, b, kt = tl[ti]
                par = tokpos[ti] % 2
                norm_tr(rows, 1, b, xn_b[par], h1_b[par], ['p2h%d' % par], 'p2n%d' % par)

            def p2_proj(ti, fl, flk, mid_hook=None):
                kind, r0, rows, b, kt = tl[ti]
                par = tokpos[ti] % 2
                h1 = h1_b[par]; hk = 'p2h%d' % par
                bank = pbk[3]; kb = KB[3]
                for k in range(8):
                    P.pe(lambda e, bank=bank, k=k, h1=h1, rows=rows:
                         e.matmul(bank[0:rows, 0:H], lhsT=h1[:, k, 0:rows], rhs=w1[:, k, 4 * D:4 * D + H],
                                  start=(k == 0), stop=(k == 7)), reads=[hk, ('w1', 8)], writes=[kb])
                P.dve(lambda e, bank=bank, fl=fl, rows=rows: e.tensor_tensor(out=fl[0:rows, 0, :], in0=bank[0:rows, 0:H],
                                                                          in1=bfb[0:rows, :], op=ALU.add),
                      reads=[kb, 'bfb'], writes=[flk])
                P.act(lambda e, fl=fl, rows=rows: e.activation(out=fl[0:rows, 1, :], in_=fl[0:rows, 0, :], func=AF.Exp, scale=-1.0),
                      reads=[flk], writes=[flk])
                P.act(lambda e, fl=fl, rows=rows: e.activation(out=fl[0:rows, 2, :], in_=fl[0:rows, 1, :], func=AF.Ln,
                                                            bias=onesf[0:rows, 0:1]),
                      reads=[flk, 'onesf'], writes=[flk])
                P.dve(lambda e, fl=fl, rows=rows: e.tensor_scalar(out=fl[0:rows, 3, :], in0=fl[0:rows, 2, :], scalar1=-1.0,
                                                               scalar2=None, op0=ALU.mult),
                      reads=[flk], writes=[flk])
                P.store(tokrows(lf_p, lf_s, r0, rows), fl[0:rows, 3, :], reads=[flk], writes=[('lf_o', r0)])
                for cb in range(8):
                    if cb == 5 and mid_hook is not None:
                        mid_hook()
                    bank = pbk[cntb['bc'] % 3]; kb = KB[cntb['bc'] % 3]
                    cntb['bc'] += 1
                    for k in range(8):
                        P.pe(lambda e, bank=bank, k=k, cb=cb, h1=h1, rows=rows:
                             e.matmul(bank[0:rows, :], lhsT=h1[:, k, 0:rows], rhs=w1[:, k, cb * 512:(cb + 1) * 512],
                                      start=(k == 0), stop=(k == 7)), reads=[hk, ('w1', cb)], writes=[kb])
                    hf = cb % 2
                    if cb < 2:
                        dstt = qb_b[par]; dk = 'p2q%d' % par
                        P.act(lambda e, bank=bank, dstt=dstt, hf=hf, rows=rows: e.activation(
                            out=dstt[0:rows, hf * 512:(hf + 1) * 512], in_=bank[0:rows, :], func=AF.Copy, scale=0.125),
                            reads=[kb], writes=[dk])
                        if hf == 1:
                            P.store(q_d[r0:r0 + rows, :], dstt[0:rows, :], reads=[dk], writes=[('q_d', r0)])
                    elif cb < 4:
                        dstt = kf_b[par]; dk = 'p2k%d' % par
                        P.dve(lambda e, bank=bank, dstt=dstt, hf=hf, rows=rows: e.tensor_copy(
                            out=dstt[0:rows, hf * 512:(hf + 1) * 512], in_=bank[0:rows, :]), reads=[kb], writes=[dk])
                        if hf == 1:
                            P.store(tokrows(k_p, k_s, r0, rows), dstt[0:rows, :], reads=[dk], writes=[('k_o', r0)])
                    elif cb < 6:
                        dstt = vf_b[par]; dk = 'p2v%d' % par
                        if hf == 0:
                            P.dve(lambda e, bank=bank, dstt=dstt, hf=hf, rows=rows: e.tensor_copy(
                                out=dstt[0:rows, hf * 512:(hf + 1) * 512], in_=bank[0:rows, :]), reads=[kb], writes=[dk])
                        else:
                            P.act(lambda e, bank=bank, dstt=dstt, hf=hf, rows=rows: e.activation(
                                out=dstt[0:rows, hf * 512:(hf + 1) * 512], in_=bank[0:rows, :], func=AF.Copy),
                                reads=[kb], writes=[dk])
                            P.store(tokrows(v_p, v_s, r0, rows), dstt[0:rows, :], reads=[dk], writes=[('v_o', r0)])
                    else:
                        dstt = zb_b[par]; dk = 'p2z%d' % par
                        P.act(lambda e, bank=bank, dstt=dstt, hf=hf, rows=rows: e.activation(
                            out=dstt[0:rows, hf * 512:(hf + 1) * 512], in_=bank[0:rows, :], func=AF.Silu),
                            reads=[kb], writes=[dk])
                        if hf == 1:
                            P.store(sz_d[r0:r0 + rows, :], dstt[0:rows, :], reads=[dk], writes=[('sz_d', r0)])
            p2_stats(toks[0])
            p2_tr(toks[0])
            p2_stats(toks[1])
            for ti, (kind, r0, rows, b, kt) in enumerate(tl):
                fl = fl_b[ti % 2]; flk = 'p2f%d' % (ti % 2)
                if kt in (0, 16, 32):
                    P.dve(lambda e: e.memset(lsum[:], 0.0), writes=['lsum'])
                if kind == 'tok':
                    nxt = tokpos[ti] + 1
                    if nxt + 1 < len(toks):
                        p2_stats(toks[nxt + 1])
                    P.flush()
                    p2_proj(ti, fl, flk, (lambda nxt=nxt: p2_tr(toks[nxt])) if nxt < len(toks) else None)
                else:
                    P.flush()
                    P.dma('sp', fl[0:rows, 3, :], clf[r0:r0 + rows, :], writes=[flk])
                bank = pbk[4 + ti % 2]; kb = KB[4 + ti % 2]
                P.pe(lambda e, bank=bank, fl=fl, rows=rows: e.matmul(bank[0:rows, 0:H], lhsT=trif[0:rows, 0:rows],
                                                                   rhs=fl[0:rows, 3, :], start=True, stop=False),
                     reads=['trif', flk], writes=[kb])
                P.pe(lambda e, bank=bank, rows=rows: e.matmul(bank[0:rows, 0:H], lhsT=onesf[:, 0:rows], rhs=lsum[:, :],
                                                            start=False, stop=True),
                     reads=['onesf', 'lsum'], writes=[kb])
                P.pe(lambda e, bank=bank: e.matmul(bank[:, H:2 * H], lhsT=onesf[:, :], rhs=lsum[:, :], start=True, stop=True),
                     reads=['onesf', 'lsum'], writes=[kb])
                P.pe(lambda e, bank=bank, fl=fl, rows=rows: e.matmul(bank[:, 2 * H:3 * H], lhsT=onesf[0:rows, :],
                                                                   rhs=fl[0:rows, 3, :], start=True, stop=True),
                     reads=['onesf', flk], writes=[kb])
                P.dve(lambda e, bank=bank, kt=kt, rows=rows: e.tensor_scalar(out=negcum[0:rows, kt, :], in0=bank[0:rows, 0:H],
                                                                           scalar1=-1.0, scalar2=None, op0=ALU.mult),
                      reads=[kb], writes=['negcum'])
                P.act(lambda e, bank=bank, kt=kt: e.activation(out=cmid[:, kt, :], in_=bank[:, H:2 * H], func=AF.Copy),
                      reads=[kb], writes=['cmid'])
                P.dve(lambda e, bank=bank, kt=kt: e.scalar_tensor_tensor(out=cmid[:, kt, :], in0=bank[:, 2 * H:3 * H], scalar=0.5,
                                                                        in1=cmid[:, kt, :], op0=ALU.mult, op1=ALU.add),
                      reads=[kb, 'cmid'], writes=['cmid'])
                P.dve(lambda e, fl=fl, rows=rows: e.tensor_tensor(out=lsum[0:rows, :], in0=lsum[0:rows, :], in1=fl[0:rows, 3, :],
                                                                op=ALU.add),
                      reads=['lsum', flk], writes=['lsum'])
                lf3 = lf3_b[ti % 2]; l3k = 'p2lf3%d' % (ti % 2)
                ct = ct_b[ti % 2]; ctk = 'p2ct%d' % (ti % 2)
                r1 = r1_b[ti % 2]; r1k = 'p2r1%d' % (ti % 2)
                hib = hib_b[ti % 2]; hik = 'p2hib%d' % (ti % 2)
                lob = lob_b[ti % 2]; lok = 'p2lob%d' % (ti % 2)
                if kt in (0, 16, 32):
                    P.dve(lambda e: e.memset(carry96[:], 0.0), writes=['carry96'])
                P.dve(lambda e, lf3=lf3, fl=fl, rows=rows: e.tensor_copy(
                    out=lf3[0:rows, :].rearrange("p (g c) -> p g c", c=32)[:, :, 0:H],
                    in_=fl[0:rows, 3:4, :].to_broadcast([rows, 3, H])), reads=[flk], writes=[l3k])
                P.pe(lambda e, bank=bank, lf3=lf3, rows=rows: e.matmul(bank[0:96, 64:64 + rows], lhsT=lf3[0:rows, 0:96],
                                                                     rhs=trif[0:rows, 0:rows], start=True, stop=True),
                     reads=[l3k, 'trif'], writes=[kb])
                P.dve(lambda e, bank=bank, ct=ct, rows=rows: e.tensor_scalar(out=ct[:, 0:rows], in0=bank[0:96, 64:64 + rows],
                                                                           scalar1=carry96[:, 0:1], scalar2=None, op0=ALU.add),
                      reads=[kb, 'carry96'], writes=[ctk])
                P.dve(lambda e, ct=ct, rows=rows: e.tensor_copy(out=carry96[:, 0:1], in_=ct[:, rows - 1:rows]),
                      reads=[ctk], writes=['carry96'])
                P.act(lambda e, ct=ct, hib=hib, rows=rows: e.activation(out=hib[:, 0:rows], in_=ct[:, 0:rows], func=AF.Copy),
                      reads=[ctk], writes=[hik])
                P.dve(lambda e, ct=ct, hib=hib, r1=r1, rows=rows: e.tensor_tensor(out=r1[:, 0:rows], in0=ct[:, 0:rows],
                                                                                in1=hib[:, 0:rows], op=ALU.subtract),
                      reads=[ctk, hik], writes=[r1k])
                P.act(lambda e, r1=r1, lob=lob, rows=rows: e.activation(out=lob[:, 0:rows], in_=r1[:, 0:rows], func=AF.Copy),
                      reads=[r1k], writes=[lok])
                P.dve(lambda e, r1=r1, lob=lob, rows=rows: e.tensor_tensor(out=r1[:, 0:rows], in0=r1[:, 0:rows],
                                                                         in1=lob[:, 0:rows], op=ALU.subtract),
                      reads=[r1k, lok], writes=[r1k])
                c0 = kt * 128
                P.act(lambda e, hib=hib, rows=rows, c0=c0: e.activation(out=augk[0:16, c0:c0 + rows], in_=hib[0:16, 0:rows],
                                                                       func=AF.Copy, scale=-1.0), reads=[hik], writes=['augk'])
                P.act(lambda e, lob=lob, rows=rows, c0=c0: e.activation(out=augk[32:48, c0:c0 + rows], in_=lob[32:48, 0:rows],
                                                                       func=AF.Copy, scale=-1.0), reads=[lok], writes=['augk'])
                P.act(lambda e, r1=r1, rows=rows, c0=c0: e.activation(out=augk[64:80, c0:c0 + rows], in_=r1[64:80, 0:rows],
                                                                      func=AF.Copy, scale=-1.0), reads=[r1k], writes=['augk'])
            P.barrier()

        if phases == 2:
            P.emit()
            return nc
        with ExitStack() as es:
            kT = sb("kT", [128, 8, 2048], BF16, es)
            vaug = sb("vaug", [128, 16, H, DH + 1], BF16, es)
            wo1 = sb("wo1", [128, 8, D], BF16, es)
            fgb = sb("fgb", [128, D], F32, es)
            kst_b = [sb("kst%d" % i, [128, D], BF16, es) for i in range(4)]
            vst_b = [sb("vst%d" % i, [128, D], BF16, es) for i in range(4)]
            qst_b = [sb("qst%d" % i, [128, D], BF16, es) for i in range(2)]
            qT_b = [sb("qT%d" % i, [128, H, 128], BF16, es) for i in range(2)]
            bias_b = [sb("bias%d" % i, [128, 16, H], F32, es) for i in range(2)]
            PT_b = [sb("PT%d" % i, [128, 512], BF16, es) for i in range(3)]
            rec = sb("rec", [128, H, 1], F32, es)
            af = sb("af", [128, H, DH], F32, es)
            szt_b = [sb("szt%d" % i, [128, D], BF16, es) for i in range(2)]
            ab = sb("ab", [128, D], BF16, es)
            aT = sb("aT", [128, 8, 128], BF16, es)
            x1t_b = [sb("x1t%d" % i, [128, D], F32, es) for i in range(2)]
            x2 = sb("x2", [128, D], F32, es)
            junk = sb("junk3", [128, D], BF16, es)
            ss3 = sb("ss3", [128, 4], F32, es)
            yt_b = [sb("yt%d" % i, [128, D], F32, es) for i in range(2)]
            P.dma('pool', wo1[:], w_out1.rearrange("(k p) n -> p k n", p=128), writes=['wo1'])
            for i2 in range(2):
                P.dve(lambda e, i2=i2: e.memset(qT_b[i2][:], 0.0), writes=['qT%d' % i2])
            P.dma('sp', fgb[:], final_g[0:1, :].to_broadcast([128, D]), writes=['fgb'])
            P.pool(lambda e: e.memset(vaug[:, :, :, DH:DH + 1], 1.0), writes=['vaug'])
            Sbanks = [0, 1, 2]
            pending = []
            scnt = 0
            kvcnt = [0]
            qc = 0
            for si in dbg.get('seqs', range(3)):
                b = si
                if si < 2:
                    keyt = [(k_p[si * 2048 + j * 128: si * 2048 + (j + 1) * 128, :],
                             v_p[si * 2048 + j * 128: si * 2048 + (j + 1) * 128, :], 128, si * 16 + j) for j in range(16)]
                    qtiles = [(si * 2048 + I * 128, 128, si * 16 + I, list(range(I + 1)), I) for I in range(16)]
                else:
                    keyt = [(ck[j * 128:(j + 1) * 128, :], cv[j * 128:(j + 1) * 128, :], 128, 32 + j) for j in range(8)]
                    keyt += [(k_s[:, :], v_s[:, :], NTS, 40)]
                    qtiles = [(NTP, NTS, 40, list(range(9)), 8)]
                def build_key(j, keyt=keyt):
                    ksrc, vsrc, rj, ktj = keyt[j]
                    kc = kvcnt[0]; kvcnt[0] += 1
                    kst = kst_b[kc % 4]; kk_ = 'kst%d' % (kc % 4)
                    vst = vst_b[kc % 4]; vk_ = 'vst%d' % (kc % 4)
                    P.dma('pool', kst[0:rj, :], ksrc, writes=[kk_])
                    P.dma('pool', vst[0:rj, :], vsrc, writes=[vk_])
                    for c in range(8):
                        P.pe(lambda e, c=c, kst=kst, rj=rj: e.transpose(out=pbT[:, c, 0:rj], in_=kst[0:rj, c * 128:(c + 1) * 128],
                                                                       identity=identb[0:rj, 0:rj]),
                             reads=[kk_, 'identb'], writes=[KT])
                    P.dve(lambda e, j=j, rj=rj: e.tensor_copy(out=kT[:, :, j * 128: j * 128 + rj], in_=pbT[:, :, 0:rj]),
                          reads=[KT], writes=['kT'])
                    P.act(lambda e, j=j, rj=rj, vst=vst: e.activation(
                        out=vaug[0:rj, j, :, 0:DH], in_=vst[0:rj, :].rearrange("p (h d) -> p h d", d=DH), func=AF.Copy),
                        reads=[vk_], writes=['vaug'])

                built = 0
                qt_list = qtiles[:dbg.get('nq', 99)]
                for qidx, (r0, rq, ktq, Js, diagJ) in enumerate(qt_list):
                    while built <= max(Js):
                        build_key(built)
                        built += 1
                    if dbg.get('stage', 9) < 1:
                        continue
                    qst = qst_b[qc % 2]; qk_ = 'qst%d' % (qc % 2)
                    qT = qT_b[qc % 2]; qTk = 'qT%d' % (qc % 2)
                    bia = bias_b[qc % 2]; bk_ = 'bias%d' % (qc % 2)
                    szt = szt_b[qc % 2]; szk_ = 'szt%d' % (qc % 2)
                    x1t = x1t_b[qc % 2]; x1k = 'x1t%d' % (qc % 2)
                    yt = yt_b[qc % 2]; ytk = 'yt%d' % (qc % 2)
                    qc += 1
                    P.dma('sp', qst[0:rq, :], q_d[r0:r0 + rq, :], writes=[qk_])
                    P.dma('sp', szt[0:rq, :], sz_d[r0:r0 + rq, :], writes=[szk_])
                    P.dma('sp', x1t[0:rq, :], x1_d[r0:r0 + rq, :], writes=[x1k])
                    P.flush()
                    for c in range(8):
                        P.pe(lambda e, c=c, qst=qst, rq=rq: e.transpose(out=pbT[:, c, 0:rq], in_=qst[0:rq, c * 128:(c + 1) * 128],
                                                                       identity=identb[0:rq, 0:rq]),
                             reads=[qk_, 'identb'], writes=[KT])
                    P.dve(lambda e, qT=qT, rq=rq: e.tensor_copy(
                        out=qT[0:64, :, 0:rq].rearrange("p (c two) q -> p c two q", two=2)[:, :, 0, :], in_=pbT[0:64, :, 0:rq]),
                        reads=[KT], writes=[qTk])
                    P.dve(lambda e, qT=qT, rq=rq: e.tensor_copy(
                        out=qT[64:128, :, 0:rq].rearrange("p (c two) q -> p c two q", two=2)[:, :, 1, :], in_=pbT[64:128, :, 0:rq]),
                        reads=[KT], writes=[qTk])
                    chunks = []
                    for h in range(dbg.get('nh', H)):
                        for c0 in range(0, len(Js), 4):
                            chunks.append((h, Js[c0:c0 + 4]))

                    def emit_S(ci, h, chunk):
                        c = h // 2
                        pb = (h % 2) * 64
                        Sb = pbk[ci % 3]; skb = KB[ci % 3]
                        PT = PT_b[ci % 3]; ptk = 'PT%d' % (ci % 3)
                        for jj, J in enumerate(chunk):
                            rj = keyt[J][2]
                            isd = (J == diagJ)
                            P.pe(lambda e, Sb=Sb, jj=jj, J=J, rj=rj, c=c, pb=pb, isd=isd, rq=rq, qT=qT, h=h:
                                 e.matmul(Sb[0:rj, jj * 128: jj * 128 + rq], lhsT=kT[:, c, J * 128: J * 128 + rj],
                                          rhs=qT[:, h, 0:rq], start=True, stop=False),
                                 reads=['kT', qTk], writes=[skb])
                            a0 = keyt[J][3] * 128
                            P.pe(lambda e, Sb=Sb, jj=jj, rj=rj, isd=isd, rq=rq, a0=a0, h=h:
                                 e.matmul(Sb[0:rj, jj * 128: jj * 128 + rq], lhsT=augk[0:96, a0:a0 + rj],
                                          rhs=selq[0:96, h, 0:rq], start=False, stop=(not isd)),
                                 reads=['augk', 'selq'], writes=[skb])
                            if isd:
                                mb = pb if rj <= 64 else 0
                                mk = maskb2 if rj <= 64 else maskb
                                P.pe(lambda e, Sb=Sb, jj=jj, rj=rj, mb=mb, mk=mk, rq=rq:
                                     e.matmul(Sb[0:rj, jj * 128: jj * 128 + rq], lhsT=identb[mb:mb + rj, mb:mb + rj],
                                              rhs=mk[mb:mb + rj, 0:rq], start=False, stop=True),
                                     reads=['identb', 'maskb', 'maskb2'], writes=[skb])
                        nJ = len(chunk)
                        rj0 = keyt[chunk[0]][2]
                        assert all(keyt[J][2] == rj0 for J in chunk)
                        if rq == 128:
                            i_ap = Sb[0:rj0, 0:nJ * 128]; o_ap = PT[0:rj0, 0:nJ * 128]
                        else:
                            i_ap = Sb[0:rj0, 0:nJ * 128].rearrange("p (j q) -> p j q", q=128)[:, :, 0:rq]
                            o_ap = PT[0:rj0, 0:nJ * 128].rearrange("p (j q) -> p j q", q=128)[:, :, 0:rq]
                        P.act(lambda e, i_ap=i_ap, o_ap=o_ap, rj0=rj0, h=h, ktq=ktq:
                              e.activation(out=o_ap, in_=i_ap, func=AF.Exp, bias=cmid[0:rj0, ktq, h:h + 1]),
                              reads=[skb, 'cmid'], writes=[ptk])

                    def emit_PV(ci, h, chunk):
                        ob = pbk[3 + h // 7]; okb = KB[3 + h // 7]
                        oslot = (h % 7) * (DH + 1)
                        PT = PT_b[ci % 3]; ptk = 'PT%d' % (ci % 3)
                        for jj, J in enumerate(chunk):
                            rj = keyt[J][2]
                            P.pe(lambda e, ob=ob, oslot=oslot, PT=PT, jj=jj, J=J, rj=rj, h=h, rq=rq, Js=Js:
                                 e.matmul(ob[0:rq, oslot:oslot + DH + 1], lhsT=PT[0:rj, jj * 128: jj * 128 + rq],
                                          rhs=vaug[0:rj, J, h, :], start=(J == Js[0]), stop=(J == Js[-1])),
                                 reads=[ptk, 'vaug'], writes=[okb])

                    base = scnt
                    LOOK = 2
                    for i in range(len(chunks) + LOOK):
                        if i == min(6, len(chunks)) and pending:
                            pending.pop(0)()
                        if i < len(chunks):
                            emit_S(base + i, *chunks[i])
                        if i >= LOOK:
                            emit_PV(base + i - LOOK, *chunks[i - LOOK])
                    scnt += len(chunks)
                    if qidx + 1 < len(qt_list):
                        while built <= max(qt_list[qidx + 1][3]):
                            build_key(built)
                            built += 1
                    if dbg.get('stage', 9) < 2:
                        continue
                    for g3 in range(3):
                        nh = 7 if g3 < 2 else 2
                        ob = pbk[3 + g3]; okb = KB[3 + g3]
                        ov = ob[:, 0:nh * (DH + 1)].rearrange("p (h c) -> p h c", c=DH + 1)
                        P.dve(lambda e, ov=ov, g3=g3, nh=nh, rq=rq: e.reciprocal(out=rec[0:rq, g3 * 7: g3 * 7 + nh, :],
                                                                                in_=ov[0:rq, :, DH:DH + 1]),
                              reads=[okb], writes=['rec'])
                        P.dve(lambda e, ov=ov, g3=g3, nh=nh, rq=rq: e.tensor_tensor(
                            out=af[0:rq, g3 * 7: g3 * 7 + nh, :], in0=ov[0:rq, :, 0:DH],
                            in1=rec[0:rq, g3 * 7: g3 * 7 + nh, :].to_broadcast([rq, nh, DH]), op=ALU.mult),
                            reads=[okb, 'rec'], writes=['af'])
                    def part_b(szt=szt, szk_=szk_, x1t=x1t, x1k=x1k, yt=yt, ytk=ytk, rq=rq, r0=r0, b=b):
                        P.pool(lambda e, szt=szt, rq=rq: e.tensor_tensor(out=ab[0:rq, :], in0=af[0:rq, :, :].rearrange("p h d -> p (h d)"),
                                                                        in1=szt[0:rq, :], op=ALU.mult),
                               reads=['af', szk_], writes=['ab'])
                        for c in range(8):
                            P.pe(lambda e, c=c, rq=rq: e.transpose(out=pbT[:, c, 0:rq], in_=ab[0:rq, c * 128:(c + 1) * 128],
                                                                  identity=identb[0:rq, 0:rq]),
                                 reads=['ab', 'identb'], writes=[KT])
                        P.dve(lambda e, rq=rq: e.tensor_copy(out=aT[:, :, 0:rq], in_=pbT[:, :, 0:rq]), reads=[KT], writes=['aT'])
                        for hf in range(2):
                            bank = pbk[6]; kb = KB[6]
                            for k in range(8):
                                P.pe(lambda e, bank=bank, k=k, hf=hf, rq=rq: e.matmul(bank[0:rq, :], lhsT=aT[:, k, 0:rq],
                                                                                    rhs=wo1[:, k, hf * 512:(hf + 1) * 512],
                                                                                    start=(k == 0), stop=(k == 7)),
                                     reads=['aT', 'wo1'], writes=[kb])
                            P.dve(lambda e, bank=bank, hf=hf, rq=rq, b=b: e.tensor_tensor(
                                out=x2[0:rq, hf * 512:(hf + 1) * 512], in0=bank[0:rq, :],
                                in1=gate_bc[0:rq, 1, b, hf * 512:(hf + 1) * 512], op=ALU.mult),
                                reads=[kb, 'gate_bc'], writes=['x2'])
                        P.pool(lambda e, x1t=x1t, rq=rq: e.tensor_tensor(out=x2[0:rq, :], in0=x2[0:rq, :], in1=x1t[0:rq, :], op=ALU.add),
                               reads=['x2', x1k], writes=['x2'])
                        P.act(lambda e, rq=rq: e.activation(out=junk[0:rq, :], in_=x2[0:rq, :], func=AF.Square, accum_out=ss3[0:rq, 0:1]),
                              reads=['x2'], writes=['junk3', 'ss3'])
                        P.pool(lambda e, rq=rq: e.tensor_scalar(out=ss3[0:rq, 1:2], in0=ss3[0:rq, 0:1], scalar1=1.0 / D, scalar2=EPS,
                                                                op0=ALU.mult, op1=ALU.add), reads=['ss3'], writes=['ss3'])
                        P.pool(lambda e, rq=rq: e.tensor_tensor(out=ss3[0:rq, 2:3], in0=ss3[0:rq, 1:2], in1=nhalf[0:rq, 0:1], op=ALU.pow),
                               reads=['ss3', 'nhalf'], writes=['ss3'])
                        P.dve(lambda e, rq=rq, yt=yt: e.scalar_tensor_tensor(out=yt[0:rq, :], in0=x2[0:rq, :], scalar=ss3[0:rq, 2:3],
                                                                            in1=fgb[0:rq, :], op0=ALU.mult, op1=ALU.mult),
                              reads=['x2', 'ss3', 'fgb'], writes=[ytk])
                        P.store(tokrows(y_p, y_s, r0, rq), yt[0:rq, :], reads=[ytk], writes=[('y_o', r0)])

                    pending.append(part_b)
                while pending:
                    pending.pop(0)()
        P.flush()
        P.emit()
    return nc


def _consts():
    ident = np.eye(128, dtype=np.float32)
    tri = np.triu(np.ones((128, 128), dtype=np.float32))
    kk = np.arange(128)[:, None]; qq = np.arange(128)[None, :]
    mask = np.where(kk <= qq, 0.0, NEG).astype(np.float32)
    mask2 = mask.copy()
    mask2[64:128, 0:64] = mask[0:64, 0:64]
    sel = np.zeros((3, 3 * 128), dtype=np.float32)
    for b in range(3):
        sel[b, b * 128:(b + 1) * 128] = 1.0
    selq = np.zeros((96, H, 128), dtype=np.float32)
    for h in range(H):
        for g in range(3):
            selq[32 * g + h, h, :] = 1.0
    return ident, tri, mask, sel, mask2, selq.reshape(96, H * 128)


_NC_CACHE = {}


def kernel(x_prompt, x_sample, c_prompt, c_sample, state_conv, cache_k, cache_v, cache_logf,
           norm_g, ada_w, ada_b, conv_w_in, conv_k, conv_w_out, attn_w_in, attn_b_f, attn_w_out, final_g):
    f = lambda a: np.ascontiguousarray(np.asarray(a, dtype=np.float32))
    ident, tri, mask, sel, mask2, selq = _consts()
    shared = {
        "norm_g": f(norm_g), "ada_w": f(ada_w), "ada_b": f(ada_b), "w_in0": f(conv_w_in[0]), "convk": f(conv_k[0]),
        "w_out0": f(conv_w_out[0]), "w_in1": f(attn_w_in[0]), "b_f": f(attn_b_f).reshape(1, H),
        "w_out1": f(attn_w_out[0]), "final_g": f(final_g).reshape(1, D),
        "c_ident": ident, "c_tri": tri, "c_mask": mask, "c_sel": sel, "c_mask2": mask2, "c_selq": selq,
    }
    in_maps = []
    for i in range(8):
        m = dict(shared)
        m["xp"] = f(x_prompt[2 * i:2 * i + 2]).reshape(NTP, D)
        m["xs"] = f(x_sample[i]).reshape(NTS, D)
        m["cc"] = f(np.stack([c_prompt[2 * i], c_prompt[2 * i + 1], c_sample[i]]))
        m["sconv"] = f(state_conv[0, i])
        m["ck"] = f(cache_k[0, i]).reshape(PAST, D)
        m["cv"] = f(cache_v[0, i]).reshape(PAST, D)
        m["clf"] = f(cache_logf[0, i])
        in_maps.append(m)
    if "nc" not in _NC_CACHE:
        _NC_CACHE["nc"] = build_nc()
    nc = _NC_CACHE["nc"]
    res = run_bass_kernel_spmd(nc, in_maps, core_ids=list(range(8)))
    R = res.results
    cat = lambda name: np.stack([np.asarray(R[i][name], dtype=np.float32) for i in range(8)])
    y_prompt = cat("y_p").reshape(16, 2048, D)
    y_sample = cat("y_s").reshape(8, NTS, D)
    new_conv_prompt = cat("conv_p").reshape(1, 16, 2, E)
    new_k_prompt = cat("k_p").reshape(1, 16, 2048, H, DH)
    new_v_prompt = cat("v_p").reshape(1, 16, 2048, H, DH)
    new_logf_prompt = cat("lf_p").reshape(1, 16, 2048, H)
    new_conv_sample = cat("conv_s").reshape(1, 8, 2, E)
    new_k_sample = cat("k_s").reshape(1, 8, NTS, H, DH)
    new_v_sample = cat("v_s").reshape(1, 8, NTS, H, DH)
    new_logf_sample = cat("lf_s").reshape(1, 8, NTS, H)
    return (y_prompt, y_sample, new_conv_prompt, new_k_prompt, new_v_prompt, new_logf_prompt,
            new_conv_sample, new_k_sample, new_v_sample, new_logf_sample)
```

```python
import numpy as np
import ml_dtypes
import concourse.bass as bass
import concourse.mybir as mybir
from concourse.bass_utils import run_bass_kernel_spmd
from contextlib import ExitStack

F32 = mybir.dt.float32
BF16 = mybir.dt.bfloat16
AF = mybir.ActivationFunctionType
ALU = mybir.AluOpType

N_DMA_SEMS = 32
D = 1024
E = 2048
NTP = 4096
NTS = 64
NTOK = NTP + NTS
PAST = 1024
H = 16
DH = 64
EPS = 1e-6
NEG = -30000.0


class Prog:
    def __init__(self, nc):
        self.nc = nc
        self.ops = []
        self.pending_st = []

    def op(self, eng, fn, reads=(), writes=(), dma=False):
        self.ops.append(dict(eng=eng, fn=fn, reads=tuple(reads), writes=tuple(writes), dma=dma,
                             deps=set(), signal=False, barrier=False))

    def pe(self, fn, reads=(), writes=()):
        self.op('pe', fn, reads, writes)

    def act(self, fn, reads=(), writes=()):
        self.op('act', fn, reads, writes)

    def dve(self, fn, reads=(), writes=()):
        self.op('dve', fn, reads, writes)

    def pool(self, fn, reads=(), writes=()):
        self.op('pool', fn, reads, writes)

    def dma(self, eng, out, in_, reads=(), writes=(), carry=False, **kw):
        self.op(eng, lambda e: e.dma_start(out=out, in_=in_, **kw), reads, writes, dma=True)
        self.ops[-1]['carry'] = carry

    def store(self, out, in_, reads=(), writes=(), **kw):
        self.pending_st.append((out, in_, tuple(reads), tuple(writes), kw))

    def flush(self):
        for out, in_, reads, writes, kw in self.pending_st:
            self.dma('sp', out, in_, reads=reads, writes=writes, **kw)
        self.pending_st = []

    def barrier(self):
        self.flush()
        for e in ('pe', 'act', 'dve', 'pool', 'sp'):
            self.ops.append(dict(eng=e, fn=None, reads=(), writes=(), dma=False, deps=set(),
                                 signal=False, barrier=True))

    def analyze(self):
        ops = self.ops
        last_writer = {}
        readers = {}
        last_on_eng = {}
        dmas_since = []
        i = 0
        n = len(ops)
        while i < n:
            op = ops[i]
            if op['barrier']:
                carried = set(d for d in dmas_since if ops[d].get('carry'))
                deps = set(last_on_eng.values()) | (set(dmas_since) - carried)
                j = i
                while j < n and ops[j]['barrier']:
                    ops[j]['deps'] = set(deps)
                    j += 1
                last_writer = {k: v for k, v in last_writer.items() if v in carried}
                readers = {}
                dmas_since = sorted(carried)
                for d in carried:
                    ops[d]['carry'] = False
                i = j
                continue
            deps = set()
            raw = set()
            for k in op['reads']:
                if k in last_writer:
                    deps.add(last_writer[k])
                    raw.add(last_writer[k])
            for k in op['writes']:
                if k in last_writer:
                    deps.add(last_writer[k])
                deps.update(readers.get(k, {}).values())
            deps.discard(i)
            op['deps'] = deps
            op['raw'] = raw
            for k in op['reads']:
                rk = ('dma', i) if op['dma'] else op['eng']
                readers.setdefault(k, {})[rk] = i
            for k in op['writes']:
                last_writer[k] = i
                readers[k] = {}
            if op['dma']:
                dmas_since.append(i)
            else:
                last_on_eng[op['eng']] = i
            i += 1
        dma_count = [0] * N_DMA_SEMS
        dma_last = [None] * N_DMA_SEMS
        jq = {'sp': 0, 'pool': 0}
        half_n = N_DMA_SEMS // 2
        for i, op in enumerate(ops):
            if op['dma']:
                qn = op['eng']
                s = (jq[qn] % half_n) + (0 if qn == 'sp' else half_n)
                jq[qn] += 1
                if dma_last[s] is not None:
                    op['deps'].add(dma_last[s])
                dma_count[s] += 16
                op['dsem'] = s
                op['dval'] = dma_count[s]
                dma_last[s] = i
        self.dma_final = dma_count
        for i, op in enumerate(ops):
            need = []
            for d in op['deps']:
                od = ops[d]
                if od['dma']:
                    need.append(d)
                    continue
                if od['eng'] == op['eng']:
                    if op['eng'] == 'pe' and not op['barrier']:
                        continue
                    if op['dma']:
                        continue
                    if not op['barrier'] and d not in op.get('raw', ()):
                        continue
                need.append(d)
                od['signal'] = True
            op['need'] = need
        cnt = {}
        for op in ops:
            if op['signal'] and not op['dma']:
                cnt[op['eng']] = cnt.get(op['eng'], 0) + 1
                op['sigval'] = cnt[op['eng']]

    def emit(self):
        nc = self.nc
        self.analyze()
        ops = self.ops
        engs = ['pe', 'act', 'dve', 'pool', 'sp']
        with ExitStack() as es:
            esem = {e: es.enter_context(nc.semaphore('prog_' + e)) for e in engs}
            dsem = [es.enter_context(nc.semaphore('dma%d' % i)) for i in range(N_DMA_SEMS)]
            block = es.enter_context(nc.Block())

            def run_engine(ename, eng):
                known = {}
                for op in ops:
                    if op['eng'] != ename:
                        continue
                    waits = {}
                    for d in op['need']:
                        od = ops[d]
                        if od['dma']:
                            key = ('d', od['dsem'])
                            val = od['dval']
                        else:
                            key = ('e', od['eng'])
                            val = od['sigval']
                        if known.get(key, 0) >= val:
                            continue
                        waits[key] = max(waits.get(key, 0), val)
                    for key, val in waits.items():
                        sem = dsem[key[1]] if key[0] == 'd' else esem[key[1]]
                        eng.wait_ge(sem, val)
                        known[key] = val
                    if op['fn'] is None:
                        continue
                    ins = op['fn'](eng)
                    if op['dma']:
                        ins.then_inc(dsem[op['dsem']], 16)
                    elif op['signal']:
                        ins.then_inc(esem[ename], 1)
                if ename == 'sp':
                    for s in range(N_DMA_SEMS):
                        if self.dma_final[s] > 0:
                            eng.wait_ge(dsem[s], self.dma_final[s])

            @block.tensor
            def _(e):
                run_engine('pe', e)

            @block.scalar
            def _(e):
                run_engine('act', e)

            @block.vector
            def _(e):
                run_engine('dve', e)

            @block.gpsimd
            def _(e):
                run_engine('pool', e)

            @block.sync
            def _(e):
                run_engine('sp', e)


def build_nc(debug=False, phases=3, dbg=None):
    dbg = dbg or {}
    nc = bass.Bass("TRN2", target_bir_lowering=False)

    def din(name, shape, dt=F32):
        return nc.dram_tensor(name, list(shape), dt, kind="ExternalInput").ap()

    def dout(name, shape, dt=F32):
        return nc.dram_tensor(name, list(shape), dt, kind="ExternalOutput").ap()

    def dscr(name, shape, dt):
        return nc.dram_tensor(name, list(shape), dt, kind=("ExternalOutput" if debug else "Internal")).ap()

    xp = din("xp", [NTP, D]); xs = din("xs", [NTS, D]); cc = din("cc", [3, D])
    sconv = din("sconv", [2, E]); ck = din("ck", [PAST, D]); cv = din("cv", [PAST, D])
    clf = din("clf", [PAST, H]); norm_g = din("norm_g", [2, D]); ada_w = din("ada_w", [2, D, 3 * D])
    ada_b = din("ada_b", [2, 3 * D]); w_in0 = din("w_in0", [D, 4 * E]); convk = din("convk", [3, E])
    w_out0 = din("w_out0", [E, D]); w_in1 = din("w_in1", [D, 4 * D + H]); b_f = din("b_f", [1, H])
    w_out1 = din("w_out1", [D, D]); final_g = din("final_g", [1, D])
    c_ident = din("c_ident", [128, 128]); c_tri = din("c_tri", [128, 128])
    c_mask = din("c_mask", [128, 128]); c_sel = din("c_sel", [3, 3 * 128]); c_mask2 = din("c_mask2", [128, 128])
    c_selq = din("c_selq", [96, H * 128])

    y_p = dout("y_p", [NTP, D]); y_s = dout("y_s", [NTS, D]); conv_p = dout("conv_p", [2, 2, E])
    k_p = dout("k_p", [NTP, D]); v_p = dout("v_p", [NTP, D]); lf_p = dout("lf_p", [NTP, H])
    conv_s = dout("conv_s", [2, E]); k_s = dout("k_s", [NTS, D]); v_s = dout("v_s", [NTS, D])
    lf_s = dout("lf_s", [NTS, H])

    x1a_d = dscr("x1a_d", [NTOK, D], F32)
    x1_d = dscr("x1_d", [NTOK, D], F32)
    q_d = dscr("q_d", [NTOK, D], BF16)
    sz_d = dscr("sz_d", [NTOK, D], BF16)

    def tokrows(ap_p, ap_s, r0, rows):
        if r0 >= NTP:
            return ap_s[r0 - NTP:r0 - NTP + rows, :]
        return ap_p[r0:r0 + rows, :]

    P = Prog(nc)
    with ExitStack() as es0:
        _uid = [0]

        def sb(name, shape, dt, es=es0):
            _uid[0] += 1
            return es.enter_context(nc.sbuf_tensor("%s_%d" % (name, _uid[0]), list(shape), dt))

        def ps(name, shape, dt, es=es0):
            return es.enter_context(nc.psum_tensor(name, list(shape), dt))

        pbT = ps("pbT", [128, 8, 128], BF16)
        pbk = [ps("pb%d" % i, [128, 512], F32) for i in range(7)]
        KT = 'psT'
        KB = ['ps%d' % i for i in range(7)]

        identf = sb("identf", [128, 128], F32)
        identb = sb("identb", [128, 128], BF16)
        trif = sb("trif", [128, 128], F32)
        onesf = sb("onesf", [128, 128], F32)
        maskb = sb("maskb", [128, 128], BF16)
        maskb2 = sb("maskb2", [128, 128], BF16)
        self3 = sb("sel3", [3, 3 * 128], F32)
        epsT = sb("epsT", [128, 1], F32)
        nhalf = sb("nhalf", [128, 1], F32)
        sc1T = sb("sc1T", [128, 2, 8, 3], F32)
        shT = sb("shT", [128, 2, 8, 3], F32)
        gate_bc = sb("gate_bc", [128, 2, 3, D], F32)
        NKT = 33 + 8
        negcum = sb("negcum", [128, NKT, H], F32)
        cbc = sb("cbc", [128, NKT + 1, H], F32)
        cmid = sb("cmid", [128, NKT, H], F32)
        augk = sb("augk", [96, NKT * 128], BF16)
        selq = sb("selq", [96, H, 128], BF16)

        P.dma('sp', identf[:], c_ident[:, :], writes=['identf'])
        P.dma('sp', trif[:], c_tri[:, :], writes=['trif'])
        P.dma('sp', self3[:], c_sel[:, :], writes=['sel3'])
        P.dma('pool', maskb[:], c_mask[:, :], writes=['maskb'])
        P.dma('pool', maskb2[:], c_mask2[:, :], writes=['maskb2'])
        P.dma('pool', selq[:].rearrange("p h q -> p (h q)"), c_selq[:, :], writes=['selq'])
        P.pool(lambda e: e.memset(augk[:], 0.0), writes=['augk'])
        P.dve(lambda e: e.tensor_copy(out=identb[:], in_=identf[:]), reads=['identf'], writes=['identb'])
        P.dve(lambda e: e.memset(onesf[:], 1.0), writes=['onesf'])
        P.dve(lambda e: e.memset(epsT[:], EPS), writes=['epsT'])
        P.pool(lambda e: e.memset(nhalf[:], -0.5), writes=['nhalf'])

        W0_CHUNKS = [(0, 128), (128, 512), (512, 1024)]
        W0_CHUNK_OF_E8 = [0, 1, 1, 1, 2, 2, 2, 2]

        def load_w0_chunk(w0, half, c2, carry):
            lo, hi = W0_CHUNKS[c2]
            for j in (2, 3, 1, 0):
                c0 = j * E + half * 1024
                P.dma('pool', w0[:, :, j, lo:hi],
                      w_in0[:, c0 + lo:c0 + hi].rearrange("(k p) n -> p k n", p=128), writes=[('w0', c2)], carry=carry)

        def load_wo0(wo0, half, carry):
            P.dma('pool', wo0[:], w_out0[half * 1024:(half + 1) * 1024, :].rearrange("(e p) n -> p e n", p=128),
                  writes=['wo0'], carry=carry)

        def load_w0(w0, wo0, half, carry):
            for c2 in range(len(W0_CHUNKS)):
                load_w0_chunk(w0, half, c2, carry)
            load_wo0(wo0, half, carry)

        es_w = ExitStack()
        w0_h0 = sb("w0", [128, 8, 4, 1024], BF16, es_w)
        wo0_h0 = sb("wo0", [128, 8, 1024], BF16, es_w)
        load_w0(w0_h0, wo0_h0, 0, True)

        with ExitStack() as es:
            cT = sb("cT", [128, 8, 3], F32, es)
            sT = sb("sT", [128, 8, 3], F32, es)
            gT = sb("gT", [128, 2, 8], F32, es)
            mod = sb("mod", [3, 2, 3 * D], F32, es)
            adab_b = [sb("adab%d" % i, [3, 512], F32, es) for i in range(2)]
            wblk = [sb("wblk%d" % i, [128, 8, 512], F32, es) for i in range(2)]
            for b3 in range(3):
                P.dma('sp', cT[:, :, b3], cc[b3, :].rearrange("(k p) -> p k", p=128), writes=['cT'],
                      allow_slow_non_contiguous=True)
            for l2 in range(2):
                P.dma('sp', gT[:, l2, :], norm_g[l2, :].rearrange("(k p) -> p k", p=128), writes=['gT'],
                      allow_slow_non_contiguous=True)
            P.act(lambda e: e.activation(out=sT[:], in_=cT[:], func=AF.Silu), reads=['cT'], writes=['sT'])
            bi = 0
            for l in range(2):
                for nb in range(6):
                    wb = wblk[bi % 2]
                    wk = 'wblk%d' % (bi % 2)
                    bi += 1
                    P.dma('sp', wb[:], ada_w[l, :, nb * 512:(nb + 1) * 512].rearrange("(k p) n -> p k n", p=128),
                          writes=[wk])
                    ab_ = adab_b[(bi - 1) % 2]; abk = 'adab%d' % ((bi - 1) % 2)
                    P.dma('sp', ab_[:], ada_b[l:l + 1, nb * 512:(nb + 1) * 512].to_broadcast([3, 512]), writes=[abk])
                    for k in range(8):
                        P.pe(lambda e, k=k, wb=wb: e.matmul(pbk[0][0:3, :], lhsT=sT[:, k, :], rhs=wb[:, k, :],
                                                          start=(k == 0), stop=(k == 7)),
                             reads=['sT', wk], writes=[KB[0]])
                    P.dve(lambda e, l=l, nb=nb, ab_=ab_: e.tensor_tensor(out=mod[:, l, nb * 512:(nb + 1) * 512],
                                                                         in0=pbk[0][0:3, :], in1=ab_[:, :], op=ALU.add),
                          reads=[KB[0], abk], writes=['mod'])
            pmod = pbk[1][:, 0:96].rearrange("p (l j b) -> p l j b", l=2, j=16, b=3)
            for l in range(2):
                for j in range(16):
                    P.pe(lambda e, l=l, j=j: e.transpose(out=pmod[:, l, j, :], in_=mod[0:3, l, j * 128:(j + 1) * 128],
                                                         identity=identf[0:3, 0:3]),
                         reads=['mod', 'identf'], writes=[KB[1]])
            for l in range(2):
                P.dve(lambda e, l=l: e.tensor_copy(out=shT[:, l, :, :], in_=pmod[:, l, 0:8, :]),
                      reads=[KB[1]], writes=['shT'])
                P.dve(lambda e, l=l: e.scalar_tensor_tensor(out=sc1T[:, l, :, :], in0=pmod[:, l, 8:16, :], scalar=1.0,
                                                            in1=gT[:, l, :].unsqueeze(2).to_broadcast([128, 8, 3]),
                                                            op0=ALU.add, op1=ALU.mult),
                      reads=[KB[1], 'gT'], writes=['sc1T'])
            for l in range(2):
                for b in range(3):
                    for hf in range(2):
                        P.pe(lambda e, l=l, b=b, hf=hf: e.matmul(pbk[2][:, :], lhsT=self3[0:3, b * 128:(b + 1) * 128],
                                                                 rhs=mod[0:3, l, 2 * D + hf * 512:2 * D + (hf + 1) * 512],
                                                                 start=True, stop=True),
                             reads=['sel3', 'mod'], writes=[KB[2]])
                        P.act(lambda e, l=l, b=b, hf=hf: e.activation(out=gate_bc[:, l, b, hf * 512:(hf + 1) * 512],
                                                                      in_=pbk[2][:, :], func=AF.Copy),
                              reads=[KB[2]], writes=['gate_bc'])
            P.barrier()

        def norm_stats(xt, rows, xn, ssq, xkey, tagn):
            P.act(lambda e: e.activation(out=xn[0:rows, :], in_=xt[0:rows, :], func=AF.Square,
                                         accum_out=ssq[0:rows, 0:1]),
                 reads=[xkey], writes=[tagn + 'xn', tagn + 'ss'])
            P.pool(lambda e: e.tensor_scalar(out=ssq[0:rows, 1:2], in0=ssq[0:rows, 0:1], scalar1=1.0 / D, scalar2=EPS,
                                             op0=ALU.mult, op1=ALU.add),
                   reads=[tagn + 'ss'], writes=[tagn + 'ss'])
            P.pool(lambda e: e.tensor_tensor(out=ssq[0:rows, 2:3], in0=ssq[0:rows, 1:2], in1=nhalf[0:rows, 0:1], op=ALU.pow),
                   reads=[tagn + 'ss', 'nhalf'], writes=[tagn + 'ss'])
            P.dve(lambda e: e.tensor_scalar(out=xn[0:rows, :], in0=xt[0:rows, :], scalar1=ssq[0:rows, 2:3],
                                            scalar2=None, op0=ALU.mult),
                  reads=[xkey, tagn + 'ss', tagn + 'xn'], writes=[tagn + 'xn'])

        def norm_tr(rows, l, b, xn, hT_dst, hkeys_w, tagn):
            for k in range(8):
                P.pe(lambda e, k=k: e.transpose(out=pbT[:, k, 0:rows], in_=xn[0:rows, k * 128:(k + 1) * 128],
                                                identity=identb[0:rows, 0:rows]),
                     reads=[tagn + 'xn', 'identb'], writes=[KT])
            for k in range(8):
                if k % 2 == 0:
                    P.act(lambda e, k=k: e.activation(out=hT_dst[:, k, 0:rows], in_=pbT[:, k, 0:rows], func=AF.Identity,
                                                      scale=sc1T[:, l, k, b:b + 1], bias=shT[:, l, k, b:b + 1]),
                          reads=[KT, 'sc1T', 'shT'], writes=hkeys_w)
                else:
                    P.dve(lambda e, k=k: e.tensor_scalar(out=hT_dst[:, k, 0:rows], in0=pbT[:, k, 0:rows],
                                                         scalar1=sc1T[:, l, k, b:b + 1], scalar2=shT[:, l, k, b:b + 1],
                                                         op0=ALU.mult, op1=ALU.add),
                          reads=[KT, 'sc1T', 'shT'], writes=hkeys_w)

        if phases == 0:
            P.emit()
            return nc
        groups = ([(g * 256, 256, 0) for g in range(8)] + [(NTP, NTS, 2)] +
                  [(g * 256, 256, 1) for g in range(8, 16)])
        es_p1 = ExitStack()
        p1 = {}
        for half in range(2):
            with ExitStack() as es:
                w0, wo0 = w0_h0, wo0_h0
                if half == 0:
                    p1['ckT'] = [sb("ckT%d" % i, [128, 3, 8], F32, es_p1) for i in range(2)]
                    p1['u_all'] = sb("u_all", [128, 8, 258], F32, es_p1)
                    p1['xt_b'] = [sb("xt%d" % i, [128, D], F32, es_p1) for i in range(2)]
                    p1['xn_b'] = [sb("xn%d" % i, [128, D], BF16, es_p1) for i in range(2)]
                    p1['ss_b'] = [sb("ss%d" % i, [128, 4], F32, es_p1) for i in range(2)]
                    p1['hT_b'] = [sb("hT%d" % i, [128, 8, 256], BF16, es_p1) for i in range(2)]
                    p1['yT_b'] = [sb("yT%d" % i, [128, 8, 256], BF16, es_p1) for i in range(2)]
                    p1['xv_b'] = [sb("xv%d" % i, [128, 256], F32, es_p1) for i in range(3)]
                    p1['sz_b'] = [sb("szz%d" % i, [128, 256], F32, es_p1) for i in range(3)]
                    p1['c1_b'] = [sb("c1%d" % i, [128, 256], F32, es_p1) for i in range(3)]
                    p1['t_b'] = [sb("tt%d" % i, [128, 256], F32, es_p1) for i in range(3)]
                    p1['xr_b'] = [sb("xr%d" % i, [128, D], F32, es_p1) for i in range(2)]
                    p1['xo_b'] = [sb("xo%d" % i, [128, D], F32, es_p1) for i in range(2)]
                ckT = p1['ckT'][half]; u_all = p1['u_all']; xt_b = p1['xt_b']; xn_b = p1['xn_b']; ss_b = p1['ss_b']
                hT_b = p1['hT_b']; yT_b = p1['yT_b']; xv_b = p1['xv_b']; sz_b = p1['sz_b']; c1_b = p1['c1_b']
                t_b = p1['t_b']; zc_b = p1['t_b']; xr_b = p1['xr_b']; xo_b = p1['xo_b']
                for j3 in range(3):
                    P.dma('sp', ckT[:, j3, :], convk[j3, half * 1024:(half + 1) * 1024].rearrange("(e p) -> p e", p=128),
                          writes=['ckT%d' % half], allow_slow_non_contiguous=True)
                src_p, src_s = (xp, xs) if half == 0 else (x1a_d[0:NTP, :], x1a_d[NTP:NTOK, :])
                dst = x1a_d if half == 0 else x1_d
                UK = [('u', e8) for e8 in range(8)]
                cnt = {'u': 0, 't': 0}

                def do_stats(gi):
                    r0, N, b = groups[gi]
                    for ti in range((N + 127) // 128):
                        rows = min(128, N - ti * 128)
                        bi = ti % 2
                        xt = xt_b[bi]; xk = 'xt%d' % bi
                        P.dma('sp', xt[0:rows, :], tokrows(xp, xs, r0 + ti * 128, rows), writes=[xk])
                        norm_stats(xt, rows, xn_b[bi], ss_b[bi], xk, 'n%d' % bi)

                def do_tr(gi, tiles=None):
                    r0, N, b = groups[gi]
                    hT = hT_b[gi % 2]; hk = 'hT%d' % (gi % 2)
                    for ti in range((N + 127) // 128):
                        if tiles is not None and ti not in tiles:
                            continue
                        rows = min(128, N - ti * 128)
                        bi = ti % 2
                        norm_tr(rows, 0, b, xn_b[bi], hT[:, :, ti * 128: ti * 128 + 128], [hk], 'n%d' % bi)

                def do_units(gi, mid_hook=None):
                    r0, N, b = groups[gi]
                    hT = hT_b[gi % 2]; hk = 'hT%d' % (gi % 2)
                    yT = yT_b[gi % 2]; yk = 'yT%d' % (gi % 2)
                    if r0 % 2048 == 0:
                        if b < 2:
                            P.dve(lambda e: e.memset(u_all[:, :, 0:2], 0.0), writes=UK)
                        else:
                            for j2 in range(2):
                                P.dma('sp', u_all[:, :, j2], sconv[j2, half * 1024:(half + 1) * 1024].rearrange("(e p) -> p e", p=128),
                                      writes=UK, allow_slow_non_contiguous=True)
                    for e8 in range(8):
                        if mid_hook is not None:
                            mid_hook(e8)
                        uc = cnt['u']; cnt['u'] += 1
                        s3 = uc % 3
                        bX = pbk[s3 * 2]; kX = KB[s3 * 2]
                        bY = pbk[s3 * 2 + 1]; kY = KB[s3 * 2 + 1]
                        uk = ('u', e8)
                        xv = xv_b[s3]; xvk = 'xv%d' % s3
                        sz = sz_b[s3]; szk = 'sz%d' % s3
                        c1 = c1_b[s3]; c1k = 'c1%d' % s3
                        tt = t_b[s3]; ttk = 'tt%d' % s3
                        zc = zc_b[s3]; zck = 'zc%d' % s3
                        for j, (bank, kb, off) in enumerate([(bY, kY, 0), (bY, kY, 256), (bX, kX, 256), (bX, kX, 0)]):
                            jj = [2, 3, 1, 0][j]
                            for k in range(8):
                                P.pe(lambda e, jj=jj, k=k, bank=bank, off=off, e8=e8, hT=hT, N=N:
                                     e.matmul(bank[:, off:off + N], lhsT=w0[:, k, jj, e8 * 128:(e8 + 1) * 128],
                                              rhs=hT[:, k, 0:N], start=(k == 0), stop=(k == 7)),
                                     reads=[('w0', W0_CHUNK_OF_E8[e8]), hk], writes=[kb])
                        P.act(lambda e, bY=bY, xv=xv, N=N: e.activation(out=xv[:, 0:N], in_=bY[:, 0:N], func=AF.Copy),
                              reads=[kY], writes=[xvk])
                        P.act(lambda e, bY=bY, sz=sz, N=N: e.activation(out=sz[:, 0:N], in_=bY[:, 256:256 + N], func=AF.Silu),
                              reads=[kY], writes=[szk])
                        P.dve(lambda e, bX=bX, xv=xv, e8=e8, N=N: e.tensor_tensor(out=u_all[:, e8, 2:2 + N], in0=bX[:, 256:256 + N],
                                                                                  in1=xv[:, 0:N], op=ALU.mult),
                              reads=[kX, xvk, uk], writes=[uk])
                        P.act(lambda e, c1=c1, e8=e8, N=N, ckT=ckT: e.activation(out=c1[:, 0:N], in_=u_all[:, e8, 2:2 + N], func=AF.Copy,
                                                                        scale=ckT[:, 2, e8:e8 + 1]),
                              reads=[uk, 'ckT%d' % half], writes=[c1k])
                        P.dve(lambda e, bX=bX, sz=sz, tt=tt, N=N: e.tensor_tensor(out=tt[:, 0:N], in0=bX[:, 0:N],
                                                                                 in1=sz[:, 0:N], op=ALU.mult),
                              reads=[kX, szk], writes=[ttk])
                        P.dve(lambda e, c1=c1, e8=e8, N=N, ckT=ckT: e.scalar_tensor_tensor(
                            out=c1[:, 0:N], in0=u_all[:, e8, 1:1 + N], scalar=ckT[:, 1, e8:e8 + 1], in1=c1[:, 0:N],
                            op0=ALU.mult, op1=ALU.add), reads=[uk, c1k, 'ckT%d' % half], writes=[c1k])
                        P.dve(lambda e, c1=c1, e8=e8, N=N, ckT=ckT: e.scalar_tensor_tensor(
                            out=c1[:, 0:N], in0=u_all[:, e8, 0:N], scalar=ckT[:, 0, e8:e8 + 1], in1=c1[:, 0:N],
                            op0=ALU.mult, op1=ALU.add), reads=[uk, c1k, 'ckT%d' % half], writes=[c1k])
                        P.dve(lambda e, tt=tt, c1=c1, yT=yT, e8=e8, N=N: e.tensor_tensor(out=yT[:, e8, 0:N], in0=tt[:, 0:N],
                                                                                        in1=c1[:, 0:N], op=ALU.mult),
                              reads=[ttk, c1k], writes=[yk])
                        P.act(lambda e, e8=e8, N=N: e.activation(out=u_all[:, e8, 0:2], in_=u_all[:, e8, N:N + 2], func=AF.Copy),
                              reads=[uk], writes=[uk])
                    if ((r0 + N) % 2048 == 0) or b == 2:
                        cdst = conv_p[b, :, half * 1024:(half + 1) * 1024] if b < 2 else conv_s[:, half * 1024:(half + 1) * 1024]
                        for j2 in range(2):
                            P.store(cdst[j2, :].rearrange("(e p) -> p e", p=128), u_all[:, :, j2], reads=UK,
                                    writes=['convout'], allow_slow_non_contiguous=True)

                def outproj_steps(gi):
                    r0, N, b = groups[gi]
                    yT = yT_b[gi % 2]; yk = 'yT%d' % (gi % 2)
                    steps = []
                    for ti in range((N + 127) // 128):
                        rows = min(128, N - ti * 128)
                        for hf in range(2):
                            def step(ti=ti, hf=hf, rows=rows):
                                if hf == 0:
                                    st['tc'] = cnt['t']; cnt['t'] += 1
                                tc = st['tc']
                                xr = xr_b[tc % 2]; xrk = 'xr%d' % (tc % 2)
                                xo = xo_b[tc % 2]; xok = 'xo%d' % (tc % 2)
                                if hf == 0:
                                    P.dma('sp', xr[0:rows, :], tokrows(src_p, src_s, r0 + ti * 128, rows), writes=[xrk],
                                          reads=([('x1a', r0 + ti * 128)] if half == 1 else []))
                                bank = pbk[6]; kb = KB[6]
                                for e8 in range(8):
                                    P.pe(lambda e, e8=e8: e.matmul(bank[0:rows, :], lhsT=yT[:, e8, ti * 128: ti * 128 + rows],
                                                                   rhs=wo0[:, e8, hf * 512:(hf + 1) * 512],
                                                                   start=(e8 == 0), stop=(e8 == 7)),
                                         reads=[yk, 'wo0'], writes=[kb])
                                P.dve(lambda e: e.tensor_tensor(out=xo[0:rows, hf * 512:(hf + 1) * 512], in0=bank[0:rows, :],
                                                                in1=gate_bc[0:rows, 0, b, hf * 512:(hf + 1) * 512], op=ALU.mult),
                                      reads=[kb, 'gate_bc'], writes=[xok])
                                if hf == 1:
                                    P.dve(lambda e: e.tensor_tensor(out=xo[0:rows, :], in0=xo[0:rows, :], in1=xr[0:rows, :],
                                                                    op=ALU.add), reads=[xok, xrk], writes=[xok])
                                    P.store(dst[r0 + ti * 128: r0 + ti * 128 + rows, :], xo[0:rows, :], reads=[xok],
                                            writes=[('x1a' if half == 0 else 'x1', r0 + ti * 128)])
                            steps.append(step)
                    return steps

                st = {'tc': 0}
                do_stats(0)
                do_tr(0)
                for gi in range(len(groups)):
                    nxt = gi + 1 < len(groups)
                    if nxt:
                        do_stats(gi + 1)
                    P.flush()

                    psteps = outproj_steps(gi - 1) if gi > 0 else []

                    def hook(e8, gi=gi, nxt=nxt, psteps=psteps):
                        slot = {1: 0, 2: 1, 3: 2, 5: 3}.get(e8)
                        if slot is not None and slot < len(psteps):
                            psteps[slot]()
                        if half == 0 and not nxt and e8 == 4:
                            load_w0_chunk(w0, 1, 0, False)
                            load_w0_chunk(w0, 1, 1, False)
                        if nxt and e8 == 4:
                            do_tr(gi + 1, (0,))
                        if nxt and e8 == 6:
                            do_tr(gi + 1, (1,))
                    do_units(gi, hook)
                if half == 0:
                    load_w0_chunk(w0, 1, 2, False)
                P.flush()
                for step in outproj_steps(len(groups) - 1):
                    step()
                if half == 0:
                    P.flush()
                    load_wo0(wo0, 1, False)
                else:
                    P.barrier()
                    es_p1.close()
                    es_w.close()
        if phases == 1:
            P.emit()
            return nc
        with ExitStack() as es:
            w1 = sb("w1", [128, 8, 4 * D + H], BF16, es)
            bfb = sb("bfb", [128, H], F32, es)
            lsum = sb("lsum", [128, H], F32, es)
            xt_b = [sb("p2xt%d" % i, [128, D], F32, es) for i in range(2)]
            xn_b = [sb("p2xn%d" % i, [128, D], BF16, es) for i in range(2)]
            ss_b = [sb("p2ss%d" % i, [128, 4], F32, es) for i in range(2)]
            h1_b = [sb("p2h%d" % i, [128, 8, 128], BF16, es) for i in range(2)]
            qb_b = [sb("p2q%d" % i, [128, D], BF16, es) for i in range(2)]
            zb_b = [sb("p2z%d" % i, [128, D], BF16, es) for i in range(2)]
            kf_b = [sb("p2k%d" % i, [128, D], F32, es) for i in range(2)]
            vf_b = [sb("p2v%d" % i, [128, D], F32, es) for i in range(2)]
            fl_b = [sb("p2f%d" % i, [128, 4, H], F32, es) for i in range(2)]
            lf3_b = [sb("p2lf3%d" % i, [128, 96], F32, es) for i in range(2)]
            ct_b = [sb("p2ct%d" % i, [96, 128], F32, es) for i in range(2)]
            r1_b = [sb("p2r1%d" % i, [96, 128], F32, es) for i in range(2)]
            hib_b = [sb("p2hib%d" % i, [96, 128], BF16, es) for i in range(2)]
            lob_b = [sb("p2lob%d" % i, [96, 128], BF16, es) for i in range(2)]
            carry96 = sb("p2carry", [96, 1], F32, es)
            for i2 in range(2):
                P.pool(lambda e, i2=i2: e.memset(lf3_b[i2][:], 0.0), writes=['p2lf3%d' % i2])
            for cb in [8] + list(range(8)):
                c0, c1_ = cb * 512, min((cb + 1) * 512, 4 * D + H)
                P.dma('pool', w1[:, :, c0:c1_], w_in1[:, c0:c1_].rearrange("(k p) n -> p k n", p=128), writes=[('w1', cb)])
            P.dma('sp', bfb[:], b_f[0:1, :].to_broadcast([128, H]), writes=['bfb'])
            tl = [('tok', t * 128, 128, t // 16, t) for t in range(32)]
            tl += [('cache', j * 128, 128, 2, 32 + j) for j in range(8)]
            tl += [('tok', NTP, NTS, 2, 40)]
            toks = [i for i, t in enumerate(tl) if t[0] == 'tok']
            tokpos = {i: n for n, i in enumerate(toks)}
            cntb = {'bc': 0}

            def p2_stats(ti):
                kind, r0, rows, b, kt = tl[ti]
                par = tokpos[ti] % 2
                xt = xt_b[par]; xk = 'p2xt%d' % par
                P.dma('sp', xt[0:rows, :], x1_d[r0:r0 + rows, :], writes=[xk])
                norm_stats(xt, rows, xn_b[par], ss_b[par], xk, 'p2n%d' % par)

            def p2_tr(ti):
                kind, r0, rows, b, kt = tl[ti]
                par = tokpos[ti] % 2
                norm_tr(rows, 1, b, xn_b[par], h1_b[par], ['p2h%d' % par], 'p2n%d' % par)

            def p2_proj(ti, fl, flk, mid_hook=None):
                kind, r0, rows, b, kt = tl[ti]
                par = tokpos[ti] % 2
                h1 = h1_b[par]; hk = 'p2h%d' % par
                bank = pbk[3]; kb = KB[3]
                for k in range(8):
                    P.pe(lambda e, bank=bank, k=k, h1=h1, rows=rows:
                         e.matmul(bank[0:rows, 0:H], lhsT=h1[:, k, 0:rows], rhs=w1[:, k, 4 * D:4 * D + H],
                                  start=(k == 0), stop=(k == 7)), reads=[hk, ('w1', 8)], writes=[kb])
                P.dve(lambda e, bank=bank, fl=fl, rows=rows: e.tensor_tensor(out=fl[0:rows, 0, :], in0=bank[0:rows, 0:H],
                                                                          in1=bfb[0:rows, :], op=ALU.add),
                      reads=[kb, 'bfb'], writes=[flk])
                P.act(lambda e, fl=fl, rows=rows: e.activation(out=fl[0:rows, 1, :], in_=fl[0:rows, 0, :], func=AF.Exp, scale=-1.0),
                      reads=[flk], writes=[flk])
                P.act(lambda e, fl=fl, rows=rows: e.activation(out=fl[0:rows, 2, :], in_=fl[0:rows, 1, :], func=AF.Ln,
                                                            bias=onesf[0:rows, 0:1]),
                      reads=[flk, 'onesf'], writes=[flk])
                P.dve(lambda e, fl=fl, rows=rows: e.tensor_scalar(out=fl[0:rows, 3, :], in0=fl[0:rows, 2, :], scalar1=-1.0,
                                                               scalar2=None, op0=ALU.mult),
                      reads=[flk], writes=[flk])
                P.store(tokrows(lf_p, lf_s, r0, rows), fl[0:rows, 3, :], reads=[flk], writes=[('lf_o', r0)])
                for cb in range(8):
                    if cb == 5 and mid_hook is not None:
                        mid_hook()
                    bank = pbk[cntb['bc'] % 3]; kb = KB[cntb['bc'] % 3]
                    cntb['bc'] += 1
                    for k in range(8):
                        P.pe(lambda e, bank=bank, k=k, cb=cb, h1=h1, rows=rows:
                             e.matmul(bank[0:rows, :], lhsT=h1[:, k, 0:rows], rhs=w1[:, k, cb * 512:(cb + 1) * 512],
                                      start=(k == 0), stop=(k == 7)), reads=[hk, ('w1', cb)], writes=[kb])
                    hf = cb % 2
                    if cb < 2:
                        dstt = qb_b[par]; dk = 'p2q%d' % par
                        P.act(lambda e, bank=bank, dstt=dstt, hf=hf, rows=rows: e.activation(
                            out=dstt[0:rows, hf * 512:(hf + 1) * 512], in_=bank[0:rows, :], func=AF.Copy, scale=0.125),
                            reads=[kb], writes=[dk])
                        if hf == 1:
                            P.store(q_d[r0:r0 + rows, :], dstt[0:rows, :], reads=[dk], writes=[('q_d', r0)])
                    elif cb < 4:
                        dstt = kf_b[par]; dk = 'p2k%d' % par
                        P.dve(lambda e, bank=bank, dstt=dstt, hf=hf, rows=rows: e.tensor_copy(
                            out=dstt[0:rows, hf * 512:(hf + 1) * 512], in_=bank[0:rows, :]), reads=[kb], writes=[dk])
                        if hf == 1:
                            P.store(tokrows(k_p, k_s, r0, rows), dstt[0:rows, :], reads=[dk], writes=[('k_o', r0)])
                    elif cb < 6:
                        dstt = vf_b[par]; dk = 'p2v%d' % par
                        if hf == 0:
                            P.dve(lambda e, bank=bank, dstt=dstt, hf=hf, rows=rows: e.tensor_copy(
                                out=dstt[0:rows, hf * 512:(hf + 1) * 512], in_=bank[0:rows, :]), reads=[kb], writes=[dk])
                        else:
                            P.act(lambda e, bank=bank, dstt=dstt, hf=hf, rows=rows: e.activation(
                                out=dstt[0:rows, hf * 512:(hf + 1) * 512], in_=bank[0:rows, :], func=AF.Copy),
                                reads=[kb], writes=[dk])
                            P.store(tokrows(v_p, v_s, r0, rows), dstt[0:rows, :], reads=[dk], writes=[('v_o', r0)])
                    else:
                        dstt = zb_b[par]; dk = 'p2z%d' % par
                        P.act(lambda e, bank=bank, dstt=dstt, hf=hf, rows=rows: e.activation(
                            out=dstt[0:rows, hf * 512:(hf + 1) * 512], in_=bank[0:rows, :], func=AF.Silu),
                            reads=[kb], writes=[dk])
                        if hf == 1:
                            P.store(sz_d[r0:r0 + rows, :], dstt[0:rows, :], reads=[dk], writes=[('sz_d', r0)])
            p2_stats(toks[0])
            p2_tr(toks[0])
            p2_stats(toks[1])
            for ti, (kind, r0, rows, b, kt) in enumerate(tl):
                fl = fl_b[ti % 2]; flk = 'p2f%d' % (ti % 2)
                if kt in (0, 16, 32):
                    P.dve(lambda e: e.memset(lsum[:], 0.0), writes=['lsum'])
                if kind == 'tok':
                    nxt = tokpos[ti] + 1
                    if nxt + 1 < len(toks):
                        p2_stats(toks[nxt + 1])
                    P.flush()
                    p2_proj(ti, fl, flk, (lambda nxt=nxt: p2_tr(toks[nxt])) if nxt < len(toks) else None)
                else:
                    P.flush()
                    P.dma('sp', fl[0:rows, 3, :], clf[r0:r0 + rows, :], writes=[flk])
                bank = pbk[4 + ti % 2]; kb = KB[4 + ti % 2]
                P.pe(lambda e, bank=bank, fl=fl, rows=rows: e.matmul(bank[0:rows, 0:H], lhsT=trif[0:rows, 0:rows],
                                                                   rhs=fl[0:rows, 3, :], start=True, stop=False),
                     reads=['trif', flk], writes=[kb])
                P.pe(lambda e, bank=bank, rows=rows: e.matmul(bank[0:rows, 0:H], lhsT=onesf[:, 0:rows], rhs=lsum[:, :],
                                                            start=False, stop=True),
                     reads=['onesf', 'lsum'], writes=[kb])
                P.pe(lambda e, bank=bank: e.matmul(bank[:, H:2 * H], lhsT=onesf[:, :], rhs=lsum[:, :], start=True, stop=True),
                     reads=['onesf', 'lsum'], writes=[kb])
                P.pe(lambda e, bank=bank, fl=fl, rows=rows: e.matmul(bank[:, 2 * H:3 * H], lhsT=onesf[0:rows, :],
                                                                   rhs=fl[0:rows, 3, :], start=True, stop=True),
                     reads=['onesf', flk], writes=[kb])
                P.dve(lambda e, bank=bank, kt=kt, rows=rows: e.tensor_scalar(out=negcum[0:rows, kt, :], in0=bank[0:rows, 0:H],
                                                                           scalar1=-1.0, scalar2=None, op0=ALU.mult),
                      reads=[kb], writes=['negcum'])
                P.act(lambda e, bank=bank, kt=kt: e.activation(out=cmid[:, kt, :], in_=bank[:, H:2 * H], func=AF.Copy),
                      reads=[kb], writes=['cmid'])
                P.dve(lambda e, bank=bank, kt=kt: e.scalar_tensor_tensor(out=cmid[:, kt, :], in0=bank[:, 2 * H:3 * H], scalar=0.5,
                                                                        in1=cmid[:, kt, :], op0=ALU.mult, op1=ALU.add),
                      reads=[kb, 'cmid'], writes=['cmid'])
                P.dve(lambda e, fl=fl, rows=rows: e.tensor_tensor(out=lsum[0:rows, :], in0=lsum[0:rows, :], in1=fl[0:rows, 3, :],
                                                                op=ALU.add),
                      reads=['lsum', flk], writes=['lsum'])
                lf3 = lf3_b[ti % 2]; l3k = 'p2lf3%d' % (ti % 2)
                ct = ct_b[ti % 2]; ctk = 'p2ct%d' % (ti % 2)
                r1 = r1_b[ti % 2]; r1k = 'p2r1%d' % (ti % 2)
                hib = hib_b[ti % 2]; hik = 'p2hib%d' % (ti % 2)
                lob = lob_b[ti % 2]; lok = 'p2lob%d' % (ti % 2)
                if kt in (0, 16, 32):
                    P.dve(lambda e: e.memset(carry96[:], 0.0), writes=['carry96'])
                P.dve(lambda e, lf3=lf3, fl=fl, rows=rows: e.tensor_copy(
                    out=lf3[0:rows, :].rearrange("p (g c) -> p g c", c=32)[:, :, 0:H],
                    in_=fl[0:rows, 3:4, :].to_broadcast([rows, 3, H])), reads=[flk], writes=[l3k])
                P.pe(lambda e, bank=bank, lf3=lf3, rows=rows: e.matmul(bank[0:96, 64:64 + rows], lhsT=lf3[0:rows, 0:96],
                                                                     rhs=trif[0:rows, 0:rows], start=True, stop=True),
                     reads=[l3k, 'trif'], writes=[kb])
                P.dve(lambda e, bank=bank, ct=ct, rows=rows: e.tensor_scalar(out=ct[:, 0:rows], in0=bank[0:96, 64:64 + rows],
                                                                           scalar1=carry96[:, 0:1], scalar2=None, op0=ALU.add),
                      reads=[kb, 'carry96'], writes=[ctk])
                P.dve(lambda e, ct=ct, rows=rows: e.tensor_copy(out=carry96[:, 0:1], in_=ct[:, rows - 1:rows]),
                      reads=[ctk], writes=['carry96'])
                P.act(lambda e, ct=ct, hib=hib, rows=rows: e.activation(out=hib[:, 0:rows], in_=ct[:, 0:rows], func=AF.Copy),
                      reads=[ctk], writes=[hik])
                P.dve(lambda e, ct=ct, hib=hib, r1=r1, rows=rows: e.tensor_tensor(out=r1[:, 0:rows], in0=ct[:, 0:rows],
                                                                                in1=hib[:, 0:rows], op=ALU.subtract),
                      reads=[ctk, hik], writes=[r1k])
                P.act(lambda e, r1=r1, lob=lob, rows=rows: e.activation(out=lob[:, 0:rows], in_=r1[:, 0:rows], func=AF.Copy),
                      reads=[r1k], writes=[lok])
                P.dve(lambda e, r1=r1, lob=lob, rows=rows: e.tensor_tensor(out=r1[:, 0:rows], in0=r1[:, 0:rows],
                                                                         in1=lob[:, 0:rows], op=ALU.subtract),
                      reads=[r1k, lok], writes=[r1k])
                c0 = kt * 128
                P.act(lambda e, hib=hib, rows=rows, c0=c0: e.activation(out=augk[0:16, c0:c0 + rows], in_=hib[0:16, 0:rows],
                                                                       func=AF.Copy, scale=-1.0), reads=[hik], writes=['augk'])
                P.act(lambda e, lob=lob, rows=rows, c0=c0: e.activation(out=augk[32:48, c0:c0 + rows], in_=lob[32:48, 0:rows],
                                                                       func=AF.Copy, scale=-1.0), reads=[lok], writes=['augk'])
                P.act(lambda e, r1=r1, rows=rows, c0=c0: e.activation(out=augk[64:80, c0:c0 + rows], in_=r1[64:80, 0:rows],
                                                                      func=AF.Copy, scale=-1.0), reads=[r1k], writes=['augk'])
            P.barrier()

        if phases == 2:
            P.emit()
            return nc
        with ExitStack() as es:
            kT = sb("kT", [128, 8, 2048], BF16, es)
            vaug = sb("vaug", [128, 16, H, DH + 1], BF16, es)
            wo1 = sb("wo1", [128, 8, D], BF16, es)
            fgb = sb("fgb", [128, D], F32, es)
            kst_b = [sb("kst%d" % i, [128, D], BF16, es) for i in range(4)]
            vst_b = [sb("vst%d" % i, [128, D], BF16, es) for i in range(4)]
            qst_b = [sb("qst%d" % i, [128, D], BF16, es) for i in range(2)]
            qT_b = [sb("qT%d" % i, [128, H, 128], BF16, es) for i in range(2)]
            bias_b = [sb("bias%d" % i, [128, 16, H], F32, es) for i in range(2)]
            PT_b = [sb("PT%d" % i, [128, 512], BF16, es) for i in range(3)]
            rec = sb("rec", [128, H, 1], F32, es)
            af = sb("af", [128, H, DH], F32, es)
            szt_b = [sb("szt%d" % i, [128, D], BF16, es) for i in range(2)]
            ab = sb("ab", [128, D], BF16, es)
            aT = sb("aT", [128, 8, 128], BF16, es)
            x1t_b = [sb("x1t%d" % i, [128, D], F32, es) for i in range(2)]
            x2 = sb("x2", [128, D], F32, es)
            junk = sb("junk3", [128, D], BF16, es)
            ss3 = sb("ss3", [128, 4], F32, es)
            yt_b = [sb("yt%d" % i, [128, D], F32, es) for i in range(2)]
            P.dma('pool', wo1[:], w_out1.rearrange("(k p) n -> p k n", p=128), writes=['wo1'])
            for i2 in range(2):
                P.dve(lambda e, i2=i2: e.memset(qT_b[i2][:], 0.0), writes=['qT%d' % i2])
            P.dma('sp', fgb[:], final_g[0:1, :].to_broadcast([128, D]), writes=['fgb'])
            P.pool(lambda e: e.memset(vaug[:, :, :, DH:DH + 1], 1.0), writes=['vaug'])
            Sbanks = [0, 1, 2]
            pending = []
            scnt = 0
            kvcnt = [0]
            qc = 0
            for si in dbg.get('seqs', range(3)):
                b = si
                if si < 2:
                    keyt = [(k_p[si * 2048 + j * 128: si * 2048 + (j + 1) * 128, :],
                             v_p[si * 2048 + j * 128: si * 2048 + (j + 1) * 128, :], 128, si * 16 + j) for j in range(16)]
                    qtiles = [(si * 2048 + I * 128, 128, si * 16 + I, list(range(I + 1)), I) for I in range(16)]
                else:
                    keyt = [(ck[j * 128:(j + 1) * 128, :], cv[j * 128:(j + 1) * 128, :], 128, 32 + j) for j in range(8)]
                    keyt += [(k_s[:, :], v_s[:, :], NTS, 40)]
                    qtiles = [(NTP, NTS, 40, list(range(9)), 8)]
                def build_key(j, keyt=keyt):
                    ksrc, vsrc, rj, ktj = keyt[j]
                    kc = kvcnt[0]; kvcnt[0] += 1
                    kst = kst_b[kc % 4]; kk_ = 'kst%d' % (kc % 4)
                    vst = vst_b[kc % 4]; vk_ = 'vst%d' % (kc % 4)
                    P.dma('pool', kst[0:rj, :], ksrc, writes=[kk_])
                    P.dma('pool', vst[0:rj, :], vsrc, writes=[vk_])
                    for c in range(8):
                        P.pe(lambda e, c=c, kst=kst, rj=rj: e.transpose(out=pbT[:, c, 0:rj], in_=kst[0:rj, c * 128:(c + 1) * 128],
                                                                       identity=identb[0:rj, 0:rj]),
                             reads=[kk_, 'identb'], writes=[KT])
                    P.dve(lambda e, j=j, rj=rj: e.tensor_copy(out=kT[:, :, j * 128: j * 128 + rj], in_=pbT[:, :, 0:rj]),
                          reads=[KT], writes=['kT'])
                    P.act(lambda e, j=j, rj=rj, vst=vst: e.activation(
                        out=vaug[0:rj, j, :, 0:DH], in_=vst[0:rj, :].rearrange("p (h d) -> p h d", d=DH), func=AF.Copy),
                        reads=[vk_], writes=['vaug'])

                built = 0
                for (r0, rq, ktq, Js, diagJ) in qtiles[:dbg.get('nq', 99)]:
                    while built <= max(Js):
                        build_key(built)
                        built += 1
                    if dbg.get('stage', 9) < 1:
                        continue
                    qst = qst_b[qc % 2]; qk_ = 'qst%d' % (qc % 2)
                    qT = qT_b[qc % 2]; qTk = 'qT%d' % (qc % 2)
                    bia = bias_b[qc % 2]; bk_ = 'bias%d' % (qc % 2)
                    szt = szt_b[qc % 2]; szk_ = 'szt%d' % (qc % 2)
                    x1t = x1t_b[qc % 2]; x1k = 'x1t%d' % (qc % 2)
                    yt = yt_b[qc % 2]; ytk = 'yt%d' % (qc % 2)
                    qc += 1
                    P.dma('sp', qst[0:rq, :], q_d[r0:r0 + rq, :], writes=[qk_])
                    P.dma('sp', szt[0:rq, :], sz_d[r0:r0 + rq, :], writes=[szk_])
                    P.dma('sp', x1t[0:rq, :], x1_d[r0:r0 + rq, :], writes=[x1k])
                    P.flush()
                    for c in range(8):
                        P.pe(lambda e, c=c, qst=qst, rq=rq: e.transpose(out=pbT[:, c, 0:rq], in_=qst[0:rq, c * 128:(c + 1) * 128],
                                                                       identity=identb[0:rq, 0:rq]),
                             reads=[qk_, 'identb'], writes=[KT])
                    P.dve(lambda e, qT=qT, rq=rq: e.tensor_copy(
                        out=qT[0:64, :, 0:rq].rearrange("p (c two) q -> p c two q", two=2)[:, :, 0, :], in_=pbT[0:64, :, 0:rq]),
                        reads=[KT], writes=[qTk])
                    P.dve(lambda e, qT=qT, rq=rq: e.tensor_copy(
                        out=qT[64:128, :, 0:rq].rearrange("p (c two) q -> p c two q", two=2)[:, :, 1, :], in_=pbT[64:128, :, 0:rq]),
                        reads=[KT], writes=[qTk])
                    chunks = []
                    for h in range(dbg.get('nh', H)):
                        for c0 in range(0, len(Js), 4):
                            chunks.append((h, Js[c0:c0 + 4]))

                    def emit_S(ci, h, chunk):
                        c = h // 2
                        pb = (h % 2) * 64
                        Sb = pbk[ci % 3]; skb = KB[ci % 3]
                        PT = PT_b[ci % 3]; ptk = 'PT%d' % (ci % 3)
                        for jj, J in enumerate(chunk):
                            rj = keyt[J][2]
                            isd = (J == diagJ)
                            P.pe(lambda e, Sb=Sb, jj=jj, J=J, rj=rj, c=c, pb=pb, isd=isd, rq=rq, qT=qT, h=h:
                                 e.matmul(Sb[0:rj, jj * 128: jj * 128 + rq], lhsT=kT[:, c, J * 128: J * 128 + rj],
                                          rhs=qT[:, h, 0:rq], start=True, stop=False),
                                 reads=['kT', qTk], writes=[skb])
                            a0 = keyt[J][3] * 128
                            P.pe(lambda e, Sb=Sb, jj=jj, rj=rj, isd=isd, rq=rq, a0=a0, h=h:
                                 e.matmul(Sb[0:rj, jj * 128: jj * 128 + rq], lhsT=augk[0:96, a0:a0 + rj],
                                          rhs=selq[0:96, h, 0:rq], start=False, stop=(not isd)),
                                 reads=['augk', 'selq'], writes=[skb])
                            if isd:
                                mb = pb if rj <= 64 else 0
                                mk = maskb2 if rj <= 64 else maskb
                                P.pe(lambda e, Sb=Sb, jj=jj, rj=rj, mb=mb, mk=mk, rq=rq:
                                     e.matmul(Sb[0:rj, jj * 128: jj * 128 + rq], lhsT=identb[mb:mb + rj, mb:mb + rj],
                                              rhs=mk[mb:mb + rj, 0:rq], start=False, stop=True),
                                     reads=['identb', 'maskb', 'maskb2'], writes=[skb])
                        nJ = len(chunk)
                        rj0 = keyt[chunk[0]][2]
                        assert all(keyt[J][2] == rj0 for J in chunk)
                        if rq == 128:
                            i_ap = Sb[0:rj0, 0:nJ * 128]; o_ap = PT[0:rj0, 0:nJ * 128]
                        else:
                            i_ap = Sb[0:rj0, 0:nJ * 128].rearrange("p (j q) -> p j q", q=128)[:, :, 0:rq]
                            o_ap = PT[0:rj0, 0:nJ * 128].rearrange("p (j q) -> p j q", q=128)[:, :, 0:rq]
                        P.act(lambda e, i_ap=i_ap, o_ap=o_ap, rj0=rj0, h=h, ktq=ktq:
                              e.activation(out=o_ap, in_=i_ap, func=AF.Exp, bias=cmid[0:rj0, ktq, h:h + 1]),
                              reads=[skb, 'cmid'], writes=[ptk])

                    def emit_PV(ci, h, chunk):
                        ob = pbk[3 + h // 7]; okb = KB[3 + h // 7]
                        oslot = (h % 7) * (DH + 1)
                        PT = PT_b[ci % 3]; ptk = 'PT%d' % (ci % 3)
                        for jj, J in enumerate(chunk):
                            rj = keyt[J][2]
                            P.pe(lambda e, ob=ob, oslot=oslot, PT=PT, jj=jj, J=J, rj=rj, h=h, rq=rq, Js=Js:
                                 e.matmul(ob[0:rq, oslot:oslot + DH + 1], lhsT=PT[0:rj, jj * 128: jj * 128 + rq],
                                          rhs=vaug[0:rj, J, h, :], start=(J == Js[0]), stop=(J == Js[-1])),
                                 reads=[ptk, 'vaug'], writes=[okb])

                    base = scnt
                    LOOK = 2
                    for i in range(len(chunks) + LOOK):
                        if i == min(6, len(chunks)) and pending:
                            pending.pop(0)()
                        if i < len(chunks):
                            emit_S(base + i, *chunks[i])
                        if i >= LOOK:
                            emit_PV(base + i - LOOK, *chunks[i - LOOK])
                    scnt += len(chunks)
                    if dbg.get('stage', 9) < 2:
                        continue
                    for g3 in range(3):
                        nh = 7 if g3 < 2 else 2
                        ob = pbk[3 + g3]; okb = KB[3 + g3]
                        ov = ob[:, 0:nh * (DH + 1)].rearrange("p (h c) -> p h c", c=DH + 1)
                        P.dve(lambda e, ov=ov, g3=g3, nh=nh, rq=rq: e.reciprocal(out=rec[0:rq, g3 * 7: g3 * 7 + nh, :],
                                                                                in_=ov[0:rq, :, DH:DH + 1]),
                              reads=[okb], writes=['rec'])
                        P.dve(lambda e, ov=ov, g3=g3, nh=nh, rq=rq: e.tensor_tensor(
                            out=af[0:rq, g3 * 7: g3 * 7 + nh, :], in0=ov[0:rq, :, 0:DH],
                            in1=rec[0:rq, g3 * 7: g3 * 7 + nh, :].to_broadcast([rq, nh, DH]), op=ALU.mult),
                            reads=[okb, 'rec'], writes=['af'])
                    def part_b(szt=szt, szk_=szk_, x1t=x1t, x1k=x1k, yt=yt, ytk=ytk, rq=rq, r0=r0, b=b):
                        P.pool(lambda e, szt=szt, rq=rq: e.tensor_tensor(out=ab[0:rq, :], in0=af[0:rq, :, :].rearrange("p h d -> p (h d)"),
                                                                        in1=szt[0:rq, :], op=ALU.mult),
                               reads=['af', szk_], writes=['ab'])
                        for c in range(8):
                            P.pe(lambda e, c=c, rq=rq: e.transpose(out=pbT[:, c, 0:rq], in_=ab[0:rq, c * 128:(c + 1) * 128],
                                                                  identity=identb[0:rq, 0:rq]),
                                 reads=['ab', 'identb'], writes=[KT])
                        P.dve(lambda e, rq=rq: e.tensor_copy(out=aT[:, :, 0:rq], in_=pbT[:, :, 0:rq]), reads=[KT], writes=['aT'])
                        for hf in range(2):
                            bank = pbk[6]; kb = KB[6]
                            for k in range(8):
                                P.pe(lambda e, bank=bank, k=k, hf=hf, rq=rq: e.matmul(bank[0:rq, :], lhsT=aT[:, k, 0:rq],
                                                                                    rhs=wo1[:, k, hf * 512:(hf + 1) * 512],
                                                                                    start=(k == 0), stop=(k == 7)),
                                     reads=['aT', 'wo1'], writes=[kb])
                            P.dve(lambda e, bank=bank, hf=hf, rq=rq, b=b: e.tensor_tensor(
                                out=x2[0:rq, hf * 512:(hf + 1) * 512], in0=bank[0:rq, :],
                                in1=gate_bc[0:rq, 1, b, hf * 512:(hf + 1) * 512], op=ALU.mult),
                                reads=[kb, 'gate_bc'], writes=['x2'])
                        P.pool(lambda e, x1t=x1t, rq=rq: e.tensor_tensor(out=x2[0:rq, :], in0=x2[0:rq, :], in1=x1t[0:rq, :], op=ALU.add),
                               reads=['x2', x1k], writes=['x2'])
                        P.act(lambda e, rq=rq: e.activation(out=junk[0:rq, :], in_=x2[0:rq, :], func=AF.Square, accum_out=ss3[0:rq, 0:1]),
                              reads=['x2'], writes=['junk3', 'ss3'])
                        P.pool(lambda e, rq=rq: e.tensor_scalar(out=ss3[0:rq, 1:2], in0=ss3[0:rq, 0:1], scalar1=1.0 / D, scalar2=EPS,
                                                                op0=ALU.mult, op1=ALU.add), reads=['ss3'], writes=['ss3'])
                        P.pool(lambda e, rq=rq: e.tensor_tensor(out=ss3[0:rq, 2:3], in0=ss3[0:rq, 1:2], in1=nhalf[0:rq, 0:1], op=ALU.pow),
                               reads=['ss3', 'nhalf'], writes=['ss3'])
                        P.dve(lambda e, rq=rq, yt=yt: e.scalar_tensor_tensor(out=yt[0:rq, :], in0=x2[0:rq, :], scalar=ss3[0:rq, 2:3],
                                                                            in1=fgb[0:rq, :], op0=ALU.mult, op1=ALU.mult),
                              reads=['x2', 'ss3', 'fgb'], writes=[ytk])
                        P.store(tokrows(y_p, y_s, r0, rq), yt[0:rq, :], reads=[ytk], writes=[('y_o', r0)])

                    pending.append(part_b)
                while pending:
                    pending.pop(0)()
        P.flush()
        P.emit()
    return nc


def _consts():
    ident = np.eye(128, dtype=np.float32)
    tri = np.triu(np.ones((128, 128), dtype=np.float32))
    kk = np.arange(128)[:, None]; qq = np.arange(128)[None, :]
    mask = np.where(kk <= qq, 0.0, NEG).astype(np.float32)
    mask2 = mask.copy()
    mask2[64:128, 0:64] = mask[0:64, 0:64]
    sel = np.zeros((3, 3 * 128), dtype=np.float32)
    for b in range(3):
        sel[b, b * 128:(b + 1) * 128] = 1.0
    selq = np.zeros((96, H, 128), dtype=np.float32)
    for h in range(H):
        for g in range(3):
            selq[32 * g + h, h, :] = 1.0
    return ident, tri, mask, sel, mask2, selq.reshape(96, H * 128)


_NC_CACHE = {}


def kernel(x_prompt, x_sample, c_prompt, c_sample, state_conv, cache_k, cache_v, cache_logf,
           norm_g, ada_w, ada_b, conv_w_in, conv_k, conv_w_out, attn_w_in, attn_b_f, attn_w_out, final_g):
    f = lambda a: np.ascontiguousarray(np.asarray(a, dtype=np.float32))
    ident, tri, mask, sel, mask2, selq = _consts()
    shared = {
        "norm_g": f(norm_g), "ada_w": f(ada_w), "ada_b": f(ada_b), "w_in0": f(conv_w_in[0]), "convk": f(conv_k[0]),
        "w_out0": f(conv_w_out[0]), "w_in1": f(attn_w_in[0]), "b_f": f(attn_b_f).reshape(1, H),
        "w_out1": f(attn_w_out[0]), "final_g": f(final_g).reshape(1, D),
        "c_ident": ident, "c_tri": tri, "c_mask": mask, "c_sel": sel, "c_mask2": mask2, "c_selq": selq,
    }
    in_maps = []
    for i in range(8):
        m = dict(shared)
        m["xp"] = f(x_prompt[2 * i:2 * i + 2]).reshape(NTP, D)
        m["xs"] = f(x_sample[i]).reshape(NTS, D)
        m["cc"] = f(np.stack([c_prompt[2 * i], c_prompt[2 * i + 1], c_sample[i]]))
        m["sconv"] = f(state_conv[0, i])
        m["ck"] = f(cache_k[0, i]).reshape(PAST, D)
        m["cv"] = f(cache_v[0, i]).reshape(PAST, D)
        m["clf"] = f(cache_logf[0, i])
        in_maps.append(m)
    if "nc" not in _NC_CACHE:
        _NC_CACHE["nc"] = build_nc()
    nc = _NC_CACHE["nc"]
    res = run_bass_kernel_spmd(nc, in_maps, core_ids=list(range(8)))
    R = res.results
    cat = lambda name: np.stack([np.asarray(R[i][name], dtype=np.float32) for i in range(8)])
    y_prompt = cat("y_p").reshape(16, 2048, D)
    y_sample = cat("y_s").reshape(8, NTS, D)
    new_conv_prompt = cat("conv_p").reshape(1, 16, 2, E)
    new_k_prompt = cat("k_p").reshape(1, 16, 2048, H, DH)
    new_v_prompt = cat("v_p").reshape(1, 16, 2048, H, DH)
    new_logf_prompt = cat("lf_p").reshape(1, 16, 2048, H)
    new_conv_sample = cat("conv_s").reshape(1, 8, 2, E)
    new_k_sample = cat("k_s").reshape(1, 8, NTS, H, DH)
    new_v_sample = cat("v_s").reshape(1, 8, NTS, H, DH)
    new_logf_sample = cat("lf_s").reshape(1, 8, NTS, H)
    return (y_prompt, y_sample, new_conv_prompt, new_k_prompt, new_v_prompt, new_logf_prompt,
            new_conv_sample, new_k_sample, new_v_sample, new_logf_sample)
```
